# Optimizing a Trainium2 kernel written in Bass

```python
import math
import jax, jax.numpy as jnp
from jax import lax
import numpy as np

D_MODEL = 1024
BATCH = 8
SEQ = 4096
DEPTH = 4

GRID_W = 64
CTX_LEN = 256
HEAD_DIM = 64
A_HEADS = 4
A_VDIM = 2 * HEAD_DIM
A_WIDTH = A_HEADS * A_VDIM
B_HEADS = 8
B_KV_HEADS = 2
B_REP = B_HEADS // B_KV_HEADS
B_WIDTH = B_HEADS * HEAD_DIM
A_Q_COLS = A_HEADS * 2 * HEAD_DIM
A_K_COLS = A_HEADS * 2 * HEAD_DIM
A_V_COLS = A_WIDTH
B_Q_COLS = B_HEADS * HEAD_DIM
B_K_COLS = B_KV_HEADS * HEAD_DIM
B_V_COLS = B_KV_HEADS * HEAD_DIM
IN_COLS = A_Q_COLS + A_K_COLS + A_V_COLS + B_Q_COLS + B_K_COLS + B_V_COLS
IN_SPLITS = (A_Q_COLS,
             A_Q_COLS + A_K_COLS,
             A_Q_COLS + A_K_COLS + A_V_COLS,
             A_Q_COLS + A_K_COLS + A_V_COLS + B_Q_COLS,
             A_Q_COLS + A_K_COLS + A_V_COLS + B_Q_COLS + B_K_COLS)
MIX_WIDTH = A_WIDTH + B_WIDTH
CONV_WIDTH = 3
MLP_HIDDEN = 4 * D_MODEL
ROPE_THETA = 10000.0
ROPE_FREQS = HEAD_DIM // 4
Q_BLOCK = 128
N_EVEN = (DEPTH + 1) // 2
N_ODD = DEPTH // 2
N_MOD = 6
DEEPNORM_ALPHA = (2 * DEPTH) ** 0.25
DEEPNORM_BETA = (8 * DEPTH) ** -0.25
EPS = 1e-6

kernel_name = "hybrid_diffattn_gqa_shortconv_dit_trunk"


def layer_norm(x, g, b):
    xf = x.astype(jnp.float32)
    mu = jnp.mean(xf, axis=-1, keepdims=True)
    var = jnp.mean(jnp.square(xf - mu), axis=-1, keepdims=True)
    return ((xf - mu) * lax.rsqrt(var + EPS) * g + b).astype(x.dtype)


def rms_norm(x, g):
    xf = x.astype(jnp.float32)
    return (xf * lax.rsqrt(jnp.mean(xf * xf, axis=-1, keepdims=True) + EPS) * g).astype(x.dtype)


def modulate(h, shift, scale):
    return h * (1.0 + scale) + shift


def rope_tables(n_tokens):
    rows = n_tokens // GRID_W
    row = jnp.broadcast_to(jnp.arange(rows)[:, None], (rows, GRID_W)).reshape(-1).astype(jnp.float32)
    col = jnp.broadcast_to(jnp.arange(GRID_W)[None, :], (rows, GRID_W)).reshape(-1).astype(jnp.float32)
    inv_freq = ROPE_THETA ** (-jnp.arange(ROPE_FREQS, dtype=jnp.float32) / ROPE_FREQS)
    ang = jnp.stack([row, col], axis=-1)[:, :, None] * inv_freq
    return jnp.cos(ang), jnp.sin(ang)


def apply_rope_2d(x, cos, sin):
    shp = x.shape
    xs = x.astype(jnp.float32).reshape(*shp[:-1], 2, 2, ROPE_FREQS)
    x1, x2 = xs[..., 0, :], xs[..., 1, :]
    n_head_axes = x.ndim - 3
    c = cos.reshape(cos.shape[0], *([1] * n_head_axes), 2, ROPE_FREQS)
    s = sin.reshape(sin.shape[0], *([1] * n_head_axes), 2, ROPE_FREQS)
    o1 = x1 * c - x2 * s
    o2 = x2 * c + x1 * s
    return jnp.stack([o1, o2], axis=-2).reshape(shp).astype(x.dtype)


def diff_attention(q, k, v, lam, lam_init, subln_g):
    s = jnp.einsum('bqhid,bkhid->bhiqk', q, k) * (HEAD_DIM ** -0.5)
    p = jax.nn.softmax(s.astype(jnp.float32), axis=-1)
    attn = (p[:, :, 0] - lam * p[:, :, 1]).astype(v.dtype)
    o = jnp.einsum('bhqk,bkhe->bqhe', attn, v)
    o = rms_norm(o, subln_g) * (1.0 - lam_init)
    return o.reshape(o.shape[0], o.shape[1], A_WIDTH)


def gqa_attention(q, k, v):
    bsz, nq = q.shape[0], q.shape[1]
    q = q.reshape(bsz, nq, B_KV_HEADS, B_REP, HEAD_DIM)
    s = jnp.einsum('bqgrd,bkgd->bgrqk', q, k) * (HEAD_DIM ** -0.5)
    p = jax.nn.softmax(s.astype(jnp.float32), axis=-1).astype(v.dtype)
    o = jnp.einsum('bgrqk,bkgd->bqgrd', p, v)
    return o.reshape(bsz, nq, B_WIDTH)


def even_mixer(h_lat, h_ctx, w_in, w_out, lam_vecs, subln_g, qn_g, kn_g, lam_init, cos, sin, ctx_out):
    bsz, n_lat = h_lat.shape[0], h_lat.shape[1]

    def project(h, use_rope):
        n = h.shape[1]
        z = h @ w_in
        aq, ak, av, bq, bk, bv = jnp.split(z, IN_SPLITS, axis=-1)
        aq = aq.reshape(bsz, n, A_HEADS, 2, HEAD_DIM)
        ak = ak.reshape(bsz, n, A_HEADS, 2, HEAD_DIM)
        av = av.reshape(bsz, n, A_HEADS, A_VDIM)
        bq = rms_norm(bq.reshape(bsz, n, B_HEADS, HEAD_DIM), qn_g)
        bk = rms_norm(bk.reshape(bsz, n, B_KV_HEADS, HEAD_DIM), kn_g)
        bv = bv.reshape(bsz, n, B_KV_HEADS, HEAD_DIM)
        if use_rope:
            aq = apply_rope_2d(aq, cos, sin)
            ak = apply_rope_2d(ak, cos, sin)
            bq = apply_rope_2d(bq, cos, sin)
            bk = apply_rope_2d(bk, cos, sin)
        return aq, ak, av, bq, bk, bv

    lv = lam_vecs.astype(jnp.float32)
    lam = jnp.exp(jnp.sum(lv[0] * lv[1])) - jnp.exp(jnp.sum(lv[2] * lv[3])) + lam_init

    caq, cak, cav, cbq, cbk, cbv = project(h_ctx, False)
    laq, lak, lav, lbq, lbk, lbv = project(h_lat, True)
    ak_all = jnp.concatenate([cak, lak], axis=1)
    av_all = jnp.concatenate([cav, lav], axis=1)
    bk_all = jnp.concatenate([cbk, lbk], axis=1)
    bv_all = jnp.concatenate([cbv, lbv], axis=1)

    nb = n_lat // Q_BLOCK

    def to_blocks(t):
        return jnp.moveaxis(t.reshape(bsz, nb, Q_BLOCK, *t.shape[2:]), 1, 0)

    def block(qs):
        qa, qb = qs
        return jnp.concatenate([diff_attention(qa, ak_all, av_all, lam, lam_init, subln_g),
                                gqa_attention(qb, bk_all, bv_all)], axis=-1)

    o = lax.map(block, (to_blocks(laq), to_blocks(lbq)))
    o = jnp.moveaxis(o, 0, 1).reshape(bsz, n_lat, MIX_WIDTH)
    y_lat = o @ w_out
    y_ctx = None
    if ctx_out:
        oc = jnp.concatenate([diff_attention(caq, cak, cav, lam, lam_init, subln_g),
                              gqa_attention(cbq, cbk, cbv)], axis=-1)
        y_ctx = oc @ w_out
    return y_lat, y_ctx


def conv_mixer(h, w_in, conv_w, conv_b, w_out):
    n = h.shape[1]
    z = h @ w_in
    gate_b, gate_c, xv = jnp.split(z, 3, axis=-1)
    u = gate_c * xv
    up = jnp.pad(u, ((0, 0), (1, 1), (0, 0)))
    y = up[:, :n] * conv_w[0] + up[:, 1:n + 1] * conv_w[1] + up[:, 2:n + 2] * conv_w[2] + conv_b
    return (gate_b * y) @ w_out


def sq_relu_mlp(h, w1, w2):
    return jnp.square(jax.nn.relu(h @ w1)) @ w2


def setup_inputs(seed: int = 0) -> dict:
    key = jax.random.key(seed)
    ks = jax.random.split(key, 20)
    f32 = jnp.float32
    nrm = lambda k, shp: jax.random.normal(k, shp, f32)
    d = D_MODEL
    return {
        "x": nrm(ks[0], (BATCH, SEQ, d)),
        "c": nrm(ks[1], (BATCH, d)),
        "ctx": nrm(ks[2], (BATCH, CTX_LEN, d)),
        "c_ctx": nrm(ks[3], (d,)),
        "ada_w": nrm(ks[4], (DEPTH, d, N_MOD * d)) * (0.5 * d ** -0.5),
        "ada_b": nrm(ks[5], (DEPTH, N_MOD * d)) * 0.02,
        "attn_w_in": nrm(ks[6], (N_EVEN, d, IN_COLS)) * d ** -0.5,
        "attn_w_out": nrm(ks[7], (N_EVEN, MIX_WIDTH, d)) * (MIX_WIDTH ** -0.5 * DEEPNORM_BETA),
        "diff_lambda": nrm(ks[8], (N_EVEN, 4, HEAD_DIM)) * 0.1,
        "diff_subln_g": 1.0 + 0.02 * nrm(ks[9], (N_EVEN, A_VDIM)),
        "q_norm_g": 1.0 + 0.02 * nrm(ks[10], (N_EVEN, HEAD_DIM)),
        "k_norm_g": 1.0 + 0.02 * nrm(ks[11], (N_EVEN, HEAD_DIM)),
        "conv_w_in": nrm(ks[12], (N_ODD, d, 3 * d)) * d ** -0.5,
        "conv_w": nrm(ks[13], (N_ODD, CONV_WIDTH, d)) * CONV_WIDTH ** -0.5,
        "conv_b": nrm(ks[14], (N_ODD, d)) * 0.02,
        "conv_w_out": nrm(ks[15], (N_ODD, d, d)) * (d ** -0.5 * DEEPNORM_BETA),
        "mlp_w1": nrm(ks[16], (DEPTH, d, MLP_HIDDEN)) * d ** -0.5,
        "mlp_w2": nrm(ks[17], (DEPTH, MLP_HIDDEN, d)) * (MLP_HIDDEN ** -0.5 * DEEPNORM_BETA),
        "ln_g": 1.0 + 0.02 * nrm(ks[18], (DEPTH, 2, d)),
        "ln_b": 0.02 * nrm(ks[19], (DEPTH, 2, d)),
    }


def reference(x, c, ctx, c_ctx, ada_w, ada_b, attn_w_in, attn_w_out, diff_lambda, diff_subln_g,
              q_norm_g, k_norm_g, conv_w_in, conv_w, conv_b, conv_w_out, mlp_w1, mlp_w2, ln_g, ln_b):
    n_lat = x.shape[1]
    cos, sin = rope_tables(n_lat)
    c_act = jax.nn.silu(c)
    cc_act = jax.nn.silu(c_ctx)
    x_ctx = ctx
    for l in range(DEPTH):
        even = (l % 2 == 0)
        ctx_out = any(j % 2 == 0 for j in range(l + 1, DEPTH))
        need_ctx = even or ctx_out
        mod = (c_act @ ada_w[l] + ada_b[l])[:, None, :]
        sh1, sc1, g1, sh2, sc2, g2 = jnp.split(mod, N_MOD, axis=-1)
        h = modulate(x, sh1, sc1)
        if need_ctx:
            mod_c = cc_act @ ada_w[l] + ada_b[l]
            csh1, csc1, cg1, csh2, csc2, cg2 = jnp.split(mod_c, N_MOD, axis=-1)
            hc = modulate(x_ctx, csh1, csc1)
        if even:
            e = l // 2
            lam_init = 0.8 - 0.6 * math.exp(-0.3 * l)
            y, yc = even_mixer(h, hc, attn_w_in[e], attn_w_out[e], diff_lambda[e], diff_subln_g[e],
                               q_norm_g[e], k_norm_g[e], lam_init, cos, sin, ctx_out)
        else:
            o = l // 2
            y = conv_mixer(h, conv_w_in[o], conv_w[o], conv_b[o], conv_w_out[o])
            yc = conv_mixer(hc, conv_w_in[o], conv_w[o], conv_b[o], conv_w_out[o]) if ctx_out else None
        x = layer_norm(DEEPNORM_ALPHA * x + g1 * y, ln_g[l, 0], ln_b[l, 0])
        h = modulate(x, sh2, sc2)
        x = layer_norm(DEEPNORM_ALPHA * x + g2 * sq_relu_mlp(h, mlp_w1[l], mlp_w2[l]), ln_g[l, 1], ln_b[l, 1])
        if ctx_out:
            x_ctx = layer_norm(DEEPNORM_ALPHA * x_ctx + cg1 * yc, ln_g[l, 0], ln_b[l, 0])
            hc = modulate(x_ctx, csh2, csc2)
            x_ctx = layer_norm(DEEPNORM_ALPHA * x_ctx + cg2 * sq_relu_mlp(hc, mlp_w1[l], mlp_w2[l]),
                               ln_g[l, 1], ln_b[l, 1])
    return x
```

```python
import numpy as np
from contextlib import ExitStack
import concourse.bass as bass
import concourse.mybir as mybir
from concourse.bass_utils import run_bass_kernel_spmd

F32 = mybir.dt.float32
BF16 = mybir.dt.bfloat16
AF = mybir.ActivationFunctionType
ALU = mybir.AluOpType
AX = mybir.AxisListType

ENGS = ("pe", "act", "dve", "pool", "sp")


class Buf:
    __slots__ = ("name", "w", "r", "excl")

    def __init__(self, name, excl=False):
        self.name = name
        self.w = None
        self.r = {}
        self.excl = excl


class Prog:
    def __init__(self, nc, es):
        self.nc = nc
        self.es = es
        self.q = {e: [] for e in ENGS}
        self.sem = {}
        self.cnt = {}
        self.seen = {e: {} for e in ENGS}
        for e in ENGS:
            self._key(e)
        self.nops = 0

    def _key(self, key):
        if key not in self.sem:
            name = "s_" + "_".join(str(x) for x in (key if isinstance(key, tuple) else (key,)))
            self.sem[key] = self.es.enter_context(self.nc.semaphore(name))
            self.cnt[key] = 0
        return key

    def op(self, eng, fns, reads=(), writes=(), dma=None):
        deps = {}

        def need(k, v):
            if deps.get(k, 0) < v:
                deps[k] = v

        for b in reads:
            if b.w is not None:
                need(*b.w)
            if b.excl:
                for k, v in b.r.items():
                    need(k, v)
        for b in writes:
            if b.w is not None:
                need(*b.w)
            for k, v in b.r.items():
                need(k, v)
        key = self._key(dma) if dma is not None else eng
        inc = 16 if dma is not None else 1
        self.cnt[key] += inc
        val = self.cnt[key]
        waits = []
        for k, v in deps.items():
            if k == eng and eng == "pe":
                continue
            if self.seen[eng].get(k, 0) >= v:
                continue
            self.seen[eng][k] = v
            waits.append((k, v))
        self.q[eng].append((waits, list(fns), key, inc))
        self.nops += len(fns)
        for b in reads:
            if b.excl:
                b.w = (key, val)
                b.r = {}
            else:
                if b.r.get(key, 0) < val:
                    b.r[key] = val
        for b in writes:
            b.w = (key, val)
            b.r = {}
        return (key, val)

    def barrier(self, engs=ENGS):
        for e in engs:
            waits = []
            for k, v in self.cnt.items():
                if v == 0:
                    continue
                if k == e:
                    continue
                if self.seen[e].get(k, 0) >= v:
                    continue
                self.seen[e][k] = v
                waits.append((k, v))
            if waits:
                self.q[e].append((waits, [], None, 0))

    def emit(self):
        nc = self.nc
        block = self.es.enter_context(nc.Block())
        sem = self.sem

        def replay(ename):
            def body(e):
                for waits, fns, key, inc in self.q[ename]:
                    for k, v in waits:
                        e.wait_ge(sem[k], v)
                    if not fns:
                        continue
                    for f in fns[:-1]:
                        f(e)
                    fns[-1](e).then_inc(sem[key], inc)
            return body

        block.tensor(replay("pe"))
        block.scalar(replay("act"))
        block.vector(replay("dve"))
        block.gpsimd(replay("pool"))
        block.sync(replay("sp"))


def MM(out, lhsT, rhs, start=True, stop=True, **kw):
    return lambda e: e.matmul(out, lhsT, rhs, start=start, stop=stop, **kw)


def TR(out, in_, ident):
    return lambda e: e.transpose(out, in_, ident)


def ACTV(out, in_, func, bias=None, scale=None, accum_out=None):
    kw = {}
    if bias is not None:
        kw["bias"] = bias
    if scale is not None:
        kw["scale"] = scale
    if accum_out is not None:
        kw["accum_out"] = accum_out
    return lambda e: e.activation(out=out, in_=in_, func=func, **kw)


def TS(out, in0, s1, s2=None, op0=ALU.mult, op1=None):
    if op1 is None:
        return lambda e: e.tensor_scalar(out=out, in0=in0, scalar1=s1, scalar2=None, op0=op0)
    return lambda e: e.tensor_scalar(out=out, in0=in0, scalar1=s1, scalar2=s2, op0=op0, op1=op1)


def TT(out, in0, in1, op):
    return lambda e: e.tensor_tensor(out=out, in0=in0, in1=in1, op=op)


def STT(out, in0, scalar, in1, op0, op1):
    return lambda e: e.scalar_tensor_tensor(out=out, in0=in0, scalar=scalar, in1=in1, op0=op0, op1=op1)


def CP(out, in_):
    return lambda e: e.tensor_copy(out=out, in_=in_)


def MEMSET(ap, v):
    return lambda e: e.memset(ap, v)


def DMA(out, in_):
    return lambda e: e.dma_start(out=out, in_=in_)


def RECIP(out, in_):
    return lambda e: e.reciprocal(out=out, in_=in_)


class Arena:
    def __init__(self, ap, nwords):
        self.ap = ap
        self.n = nwords
        self.off = 0
        self.hi = 0

    def mark(self):
        return self.off

    def reset(self, off):
        self.off = off

    def alloc(self, shape, dtype=F32):
        n = int(np.prod(shape))
        words = n if dtype == F32 else (n + 1) // 2
        words = (words + 15) // 16 * 16
        assert self.off + words <= self.n, f"SBUF arena overflow {self.off}+{words}>{self.n}"
        v = self.ap[:, self.off:self.off + words]
        self.off += words
        self.hi = max(self.hi, self.off)
        if dtype != F32:
            v = v.bitcast(dtype)
        v = v[:, 0:n]
        if len(shape) == 1:
            return v
        names = " ".join(f"d{i}" for i in range(len(shape)))
        kw = {f"d{i}": int(s) for i, s in enumerate(shape)}
        return v.rearrange(f"p ({names}) -> p {names}", **kw)


D = 1024
DC = 8
S = 4096
CTX = 256
NTOK = S + CTX
NKT = NTOK // 128
H = 4096
HC = 32
DEPTH = 4
ALPHA = float((2 * DEPTH) ** 0.25)
EPS = 1e-6
VA_W = 4 * 129 + 2 * 65
NWORDS = 53000


class Ring:
    def __init__(self, items):
        self.items = items
        self.i = 0

    def next(self):
        it = self.items[self.i % len(self.items)]
        self.i += 1
        return it


class Stream:
    def __init__(self, ap, name, row_off=0):
        self.ap = ap
        self.bufs = [Buf(f"{name}{t}") for t in range(NKT)]
        self.row_off = row_off

    def rows(self, tile):
        r = tile * 128 - self.row_off
        return self.ap[r:r + 128, :]


def build(stop_after=None, debug=False):
    nc = bass.Bass("TRN2", target_bir_lowering=False)

    def din(name, shape, dt=F32):
        return nc.dram_tensor(name, list(shape), dt, kind="ExternalInput").ap()

    xs = din("xs", [NTOK, D])
    cvec = din("cvec", [128, 16])
    ada_w = din("ada_w", [4, D, 6 * D])
    ada_b = din("ada_b", [4, 6 * D])
    w_in = din("attn_w_in", [2, D, 2304])
    w_out = din("attn_w_out", [2, D, D])
    dlam = din("diff_lambda", [2, 256])
    subg = din("diff_subln_g", [2, 128])
    qng = din("q_norm_g", [2, 64])
    kng = din("k_norm_g", [2, 64])
    cw_in = din("conv_w_in", [2, D, 3 * D])
    cw_out = din("conv_w_out", [2, D, D])
    convT_d = din("convT", [128, 64])
    w1_d = din("mlp_w1", [4, D, H])
    w2_d = din("mlp_w2", [4, H, D])
    lnT_d = din("lnT", [128, 128])
    lng_d = din("ln_g", [4, 2, D])
    lnb_d = din("ln_b", [4, 2, D])
    ropeC = din("ropeC", [NTOK, 64])
    ropeS = din("ropeS", [NTOK, 64])
    ident_d = din("ident", [128, 128])
    out_d = nc.dram_tensor("out", [S, D], F32, kind="ExternalOutput").ap()
    skind = "ExternalOutput" if debug else "Internal"
    scrA = nc.dram_tensor("scrA", [NTOK, D], F32, kind=skind).ap()
    scrB = nc.dram_tensor("scrB", [NTOK, D], F32, kind=skind).ap()

    es = ExitStack()
    with es:
        arena_t = es.enter_context(nc.sbuf_tensor("arena", [128, NWORDS], F32))
        ps = es.enter_context(nc.psum_tensor("ps", [128, 4096], F32))
        es.enter_context(nc.allow_low_precision("bf16 matmul operands, fp32 accumulation"))
        A = Arena(arena_t, NWORDS)
        P = Prog(nc, es)
        banks = [ps[:, b * 512:(b + 1) * 512] for b in range(8)]
        bbuf = [Buf(f"bank{b}", excl=True) for b in range(8)]
        BK = [(banks[b], bbuf[b]) for b in range(8)]
        pair_ap = ps[:, 2 * 512:4 * 512]
        pair_bufs = [bbuf[2], bbuf[3]]

        S_in = Stream(xs, "xs")
        S_A = Stream(scrA, "sa")
        S_B = Stream(scrB, "sb")
        S_out = Stream(out_d, "so", row_off=CTX)

        identF = A.alloc([128]); identB = A.alloc([128], BF16); b_id = Buf("ident")
        cs = A.alloc([16]); csA = A.alloc([8, 2]); b_cs = Buf("cs")
        modT = A.alloc([4, 48, 2]); b_mod = Buf("modT")
        lnT = A.alloc([4, 2, 2, 8]); convT = A.alloc([2, 4, 8]); b_par = Buf("params")
        neghalf = A.alloc([16]); ones8 = A.alloc([8]); zeros8 = A.alloc([8]); b_const = Buf("const")
        V = {}
        for nm in ("A1", "B1", "g1", "A2", "B2", "g2"):
            V[nm] = [A.alloc([8]), A.alloc([8])]
        for nm in ("XA1", "XB1", "XA2", "XB2", "tmp"):
            V[nm] = A.alloc([8])
        b_vec = Buf("vecs")
        stat_ring = Ring([dict(st=A.alloc([2, 6]), mv=A.alloc([2]), ve=A.alloc([1]), rstd=A.alloc([1]),
                               nb=A.alloc([1]), buf=Buf(f"stat{i}")) for i in range(4)])

        P.op("sp", [DMA(identF, ident_d)], writes=[b_id], dma=("c", 0))
        P.op("pool", [DMA(identB, ident_d)], writes=[b_id], dma=("c", 1))
        P.op("sp", [DMA(cs, cvec)], writes=[b_cs], dma=("c", 2))
        P.op("sp", [DMA(lnT.rearrange("p a b c d -> p (a b c d)"), lnT_d)], writes=[b_par], dma=("c", 3))
        P.op("sp", [DMA(convT.rearrange("p a b c -> p (a b c)"), convT_d)], writes=[b_par], dma=("c", 4))
        P.op("pool", [MEMSET(neghalf, -0.5), MEMSET(ones8, 1.0), MEMSET(zeros8, 0.0)], writes=[b_const])
        P.op("act", [ACTV(csA[:, :, 0], cs[:, 0:8], AF.Silu), ACTV(csA[:, :, 1], cs[:, 8:16], AF.Silu)],
             reads=[b_cs], writes=[b_cs])

        m0 = A.mark()
        adab2 = A.alloc([6 * D]); b_adab = Buf("adab")
        modrow = A.alloc([6 * D]); b_modrow = Buf("modrow")
        wring = Ring([(A.alloc([8, 512]), Buf(f"adaw{i}"), i) for i in range(3)])
        for l in range(DEPTH):
            for r in range(2):
                P.op("sp", [DMA(adab2[r:r + 1, :], ada_b[l:l + 1, :])], writes=[b_adab], dma=("adab", r))
            for nb in range(12):
                wt, wb, wi = wring.next()
                P.op("sp", [DMA(wt, ada_w[l].rearrange("(k p) n -> p k n", p=128)[:, :, nb * 512:(nb + 1) * 512])],
                     writes=[wb], dma=("adaw", wi))
                bk, bb_ = BK[nb % 2]
                P.op("pe", [MM(bk[0:2, :], csA[:, k, :], wt[:, k, :], start=(k == 0), stop=(k == 7)) for k in range(8)],
                     reads=[wb, b_cs], writes=[bb_])
                P.op("dve", [TT(modrow[0:2, nb * 512:(nb + 1) * 512], bk[0:2, :], adab2[0:2, nb * 512:(nb + 1) * 512], ALU.add)],
                     reads=[bb_, b_adab], writes=[b_modrow])
            bk, bb_ = BK[4]
            P.op("pe", [TR(bk[:, 2 * i:2 * i + 2], modrow[0:2, i * 128:(i + 1) * 128], identF[0:2, 0:2]) for i in range(48)],
                 reads=[b_modrow, b_id], writes=[bb_])
            P.op("dve", [CP(modT[:, l].rearrange("p a b -> p (a b)"), bk[:, 0:96])], reads=[bb_], writes=[b_mod])
        P.barrier()
        A.reset(m0)
        if stop_after == "prologue":
            P.emit()
            return nc

        def mvec(l, j, s):
            return modT[:, l, j * 8:(j + 1) * 8, s]

        def layer_vectors(l):
            if l == 0:
                Gp, Bp = ones8, zeros8
            else:
                Gp, Bp = lnT[:, l - 1, 1, 0, :], lnT[:, l - 1, 1, 1, :]
            G2, B2 = lnT[:, l, 0, 0, :], lnT[:, l, 0, 1, :]
            ops = []
            for s in range(2):
                sh1, sc1, g1, sh2, sc2, g2 = [mvec(l, j, s) for j in range(6)]
                ops += [TS(V["tmp"], sc1, 1.0, None, ALU.add),
                        TT(V["A1"][s], V["tmp"], Gp, ALU.mult),
                        TT(V["B1"][s], V["tmp"], Bp, ALU.mult),
                        TT(V["B1"][s], V["B1"][s], sh1, ALU.add),
                        CP(V["g1"][s], g1),
                        TS(V["tmp"], sc2, 1.0, None, ALU.add),
                        TT(V["A2"][s], V["tmp"], G2, ALU.mult),
                        TT(V["B2"][s], V["tmp"], B2, ALU.mult),
                        TT(V["B2"][s], V["B2"][s], sh2, ALU.add),
                        CP(V["g2"][s], g2)]
            ops += [TS(V["XA1"], Gp, ALPHA, None, ALU.mult), TS(V["XB1"], Bp, ALPHA, None, ALU.mult),
                    TS(V["XA2"], G2, ALPHA, None, ALU.mult), TS(V["XB2"], B2, ALPHA, None, ALU.mult)]
            for o in ops:
                P.op("dve", [o], reads=[b_mod, b_par, b_const, b_vec], writes=[b_vec])

        def head(src, tile0, ntok, s, which, hT, b_hT, xaT, b_xaT, xin, ring):
            nt = ntok // 128
            Av, Bv = V["A" + which][s], V["B" + which][s]
            XAv, XBv = V["XA" + which], V["XB" + which]
            for t in range(nt):
                xt, xb, xi = xin[t]
                P.op("sp", [DMA(xt, src.rows(tile0 + t))], reads=[src.bufs[tile0 + t]], writes=[xb], dma=("xin", xi))
            for c in range(8):
                bk, bb_ = ring.next()
                P.op("pe", [TR(bk[:, t * 128:(t + 1) * 128], xin[t][0][:, c * 128:(c + 1) * 128], identF) for t in range(nt)],
                     reads=[xin[t][1] for t in range(nt)] + [b_id], writes=[bb_])
                P.op("act", [ACTV(hT[:, c, 0:ntok], bk[:, 0:ntok], AF.Identity, bias=Bv[:, c:c + 1], scale=Av[:, c:c + 1])],
                     reads=[bb_, b_vec], writes=[b_hT])
                if xaT is not None:
                    P.op("dve", [TS(xaT[:, c, 0:ntok], bk[:, 0:ntok], XAv[:, c:c + 1], XBv[:, c:c + 1], ALU.mult, ALU.add)],
                         reads=[bb_, b_vec], writes=[b_xaT])

        def tail(rT, b_rT, ntok, dst, tile0, xout, final_tabs=None):
            nt = ntok // 128
            for t in range(nt):
                P.op("pe", [TR(pair_ap[:, c * 128:(c + 1) * 128], rT[:, c, t * 128:(t + 1) * 128], identF) for c in range(8)],
                     reads=[b_rT, b_id], writes=pair_bufs)
                st = stat_ring.next()
                P.op("dve", [lambda e, st=st: e.bn_stats(out=st["st"][:, 0, :], in_=pair_ap[:, 0:512]),
                             lambda e, st=st: e.bn_stats(out=st["st"][:, 1, :], in_=pair_ap[:, 512:1024])],
                     reads=pair_bufs, writes=[st["buf"]])
                P.op("dve", [lambda e, st=st: e.bn_aggr(out=st["mv"], in_=st["st"].rearrange("p a b -> p (a b)"))],
                     reads=[st["buf"]], writes=[st["buf"]])
                P.op("dve", [TS(st["ve"], st["mv"][:, 1:2], EPS, None, ALU.add)], reads=[st["buf"]], writes=[st["buf"]])
                P.op("pool", [TT(st["rstd"], st["ve"], neghalf[:, 0:1], ALU.pow)], reads=[st["buf"], b_const], writes=[st["buf"]])
                P.op("dve", [STT(st["nb"], st["mv"][:, 0:1], -1.0, st["rstd"], ALU.mult, ALU.mult)],
                     reads=[st["buf"]], writes=[st["buf"]])
                xo, xob, xoi = xout.next()
                P.op("act", [ACTV(xo, pair_ap, AF.Identity, bias=st["nb"][:, 0:1], scale=st["rstd"][:, 0:1])],
                     reads=pair_bufs + [st["buf"]], writes=[xob])
                if final_tabs is not None:
                    gt, bt, b_tab = final_tabs
                    P.op("pool", [TT(xo, xo, gt, ALU.mult)], reads=[xob, b_tab], writes=[xob])
                    P.op("pool", [TT(xo, xo, bt, ALU.add)], reads=[xob, b_tab], writes=[xob])
                P.op("sp", [DMA(dst.rows(tile0 + t), xo)], reads=[xob], writes=[dst.bufs[tile0 + t]], dma=("xout", xoi))

        def load_rows_bf16(dst, src_rows, kgroups, key, bufs):
            pass

        LAT512 = [(2 + 4 * b, 512, 0) for b in range(8)]
        LAT256 = [(2 + 2 * b, 256, 0) for b in range(16)]
        CTXB = [(0, 256, 1)]

        def phase_mlp(l, src, dst, blocks, final):
            m = A.mark()
            NT = 256
            w1 = A.alloc([8, H], BF16); w1b = [Buf(f"w1_{i}") for i in range(8)]
            w2 = A.alloc([HC, D], BF16); w2b = [Buf(f"w2_{i}") for i in range(8)]
            hid = A.alloc([HC, NT], BF16); hidb = [Buf(f"hid{j}") for j in range(HC)]
            hT = A.alloc([8, NT], BF16); b_hT = Buf("hT")
            xaT = A.alloc([8, NT]); b_xaT = Buf("xaT")
            xin = [(A.alloc([D]), Buf(f"xin{i}"), i) for i in range(2)]
            xout = Ring([(A.alloc([D]), Buf(f"xout{i}"), i) for i in range(2)])
            sq = Ring([(A.alloc([NT]), Buf(f"sq{i}")) for i in range(3)])
            ring = Ring([BK[i] for i in (0, 1, 4, 5, 6, 7)])
            tabs = None
            if final:
                gt = A.alloc([D]); bt = A.alloc([D]); b_tab = Buf("ftab")
                P.op("sp", [DMA(gt, lng_d[l, 1].partition_broadcast(128))], writes=[b_tab], dma=("ftab", 0))
                P.op("sp", [DMA(bt, lnb_d[l, 1].partition_broadcast(128))], writes=[b_tab], dma=("ftab", 1))
                tabs = (gt, bt, b_tab)
            w1v = w1_d[l].rearrange("(k p) n -> p k n", p=128)
            w2v = w2_d[l].rearrange("(j p) n -> p j n", p=128)
            for nb in range(8):
                P.op("pool", [DMA(w1[:, :, nb * 512:(nb + 1) * 512], w1v[:, :, nb * 512:(nb + 1) * 512])],
                     writes=[w1b[nb]], dma=("w1", nb))
            for jg in range(8):
                P.op("pool", [DMA(w2[:, jg * 4:(jg + 1) * 4, :], w2v[:, jg * 4:(jg + 1) * 4, :])],
                     writes=[w2b[jg]], dma=("w2", jg))
            for (tile0, ntok, s) in blocks:
                head(src, tile0, ntok, s, "2", hT, b_hT, xaT, b_xaT, xin, ring)
                for j in range(HC):
                    bk, bb_ = ring.next()
                    P.op("pe", [MM(bk[:, 0:ntok], w1[:, k, j * 128:(j + 1) * 128], hT[:, k, 0:ntok], start=(k == 0), stop=(k == 7))
                                for k in range(8)], reads=[b_hT, w1b[j // 4]], writes=[bb_])
                    sqt, sqb = sq.next()
                    P.op("act", [ACTV(sqt[:, 0:ntok], bk[:, 0:ntok], AF.Square)], reads=[bb_], writes=[sqb])
                    P.op("dve", [STT(hid[:, j, 0:ntok], bk[:, 0:ntok], 0.0, sqt[:, 0:ntok], ALU.is_gt, ALU.mult)],
                         reads=[bb_, sqb], writes=[hidb[j]])
                g2 = V["g2"][s]
                for c in range(8):
                    bk, bb_ = ring.next()
                    P.op("pe", [MM(bk[:, 0:ntok], w2[:, j, c * 128:(c + 1) * 128], hid[:, j, 0:ntok], start=(j == 0), stop=(j == HC - 1))
                                for j in range(HC)], reads=hidb + w2b, writes=[bb_])
                    P.op("dve", [STT(xaT[:, c, 0:ntok], bk[:, 0:ntok], g2[:, c:c + 1], xaT[:, c, 0:ntok], ALU.mult, ALU.add)],
                         reads=[bb_, b_xaT, b_vec], writes=[b_xaT])
                dtile0 = tile0
                tail(xaT, b_xaT, ntok, dst, dtile0, xout, tabs)
            P.barrier()
            A.reset(m)

        def phase_conv(l, src, dst, seqs):
            o = l // 2
            m = A.mark()
            NT = 256
            wci = A.alloc([8, 3 * D], BF16); wcib = [Buf(f"wci{i}") for i in range(6)]
            wco = A.alloc([8, D], BF16); wcob = Buf("wco")
            hT = A.alloc([8, NT], BF16); b_hT = Buf("hT")
            xa_ring = Ring([(A.alloc([8, NT]), Buf(f"xaT{i}")) for i in range(2)])
            gb_ring = Ring([(A.alloc([8, NT]), Buf(f"gb{i}")) for i in range(2)])
            u_ring = Ring([(A.alloc([8, NT + 2]), Buf(f"u{i}")) for i in range(3)])
            gct = Ring([(A.alloc([NT]), Buf(f"gct{i}")) for i in range(2)])
            acc_t = Ring([(A.alloc([NT]), Buf(f"cacc{i}")) for i in range(2)])
            vT = A.alloc([8, NT], BF16); b_vT = Buf("vT")
            xin = [(A.alloc([D]), Buf(f"xin{i}"), i) for i in range(2)]
            xout = Ring([(A.alloc([D]), Buf(f"xout{i}"), i) for i in range(2)])
            ring = Ring([BK[i] for i in (0, 1, 4, 5, 6, 7)])
            wv = cw_in[o].rearrange("(k p) n -> p k n", p=128)
            for nb in range(6):
                P.op("pool", [DMA(wci[:, :, nb * 512:(nb + 1) * 512], wv[:, :, nb * 512:(nb + 1) * 512])],
                     writes=[wcib[nb]], dma=("wci", nb))
            P.op("pool", [DMA(wco, cw_out[o].rearrange("(k p) n -> p k n", p=128))], writes=[wcob], dma=("wco", 0))
            cw0, cw1, cw2, cbv = [convT[:, o, i, :] for i in range(4)]

            def proj(blk):
                tile0, ntok, s = blk
                xaT, b_xaT = xa_ring.next()
                gb, b_gb = gb_ring.next()
                u, b_u = u_ring.next()
                head(src, tile0, ntok, s, "1", hT, b_hT, xaT, b_xaT, xin, ring)
                for c in range(8):
                    bk, bb_ = ring.next()
                    P.op("pe", [MM(bk[:, 0:ntok], wci[:, k, c * 128:(c + 1) * 128], hT[:, k, 0:ntok], start=(k == 0), stop=(k == 7))
                                for k in range(8)], reads=[b_hT, wcib[c // 4]], writes=[bb_])
                    P.op("act", [ACTV(gb[:, c, 0:ntok], bk[:, 0:ntok], AF.Copy)], reads=[bb_], writes=[b_gb])
                    bk, bb_ = ring.next()
                    cc = 8 + c
                    P.op("pe", [MM(bk[:, 0:ntok], wci[:, k, cc * 128:(cc + 1) * 128], hT[:, k, 0:ntok], start=(k == 0), stop=(k == 7))
                                for k in range(8)], reads=[b_hT, wcib[cc // 4]], writes=[bb_])
                    g_t, g_b = gct.next()
                    P.op("act", [ACTV(g_t[:, 0:ntok], bk[:, 0:ntok], AF.Copy)], reads=[bb_], writes=[g_b])
                    bk, bb_ = ring.next()
                    cc = 16 + c
                    P.op("pe", [MM(bk[:, 0:ntok], wci[:, k, cc * 128:(cc + 1) * 128], hT[:, k, 0:ntok], start=(k == 0), stop=(k == 7))
                                for k in range(8)], reads=[b_hT, wcib[cc // 4]], writes=[bb_])
                    P.op("dve", [TT(u[:, c, 1:ntok + 1], bk[:, 0:ntok], g_t[:, 0:ntok], ALU.mult)], reads=[bb_, g_b], writes=[b_u])
                return dict(blk=blk, xaT=xaT, b_xaT=b_xaT, gb=gb, b_gb=b_gb, u=u, b_u=b_u)

            def convout(cur, prev, nxt):
                tile0, ntok, s = cur["blk"]
                u, b_u = cur["u"], cur["b_u"]
                if prev is None:
                    P.op("pool", [MEMSET(u[:, :, 0:1], 0.0)], writes=[b_u])
                else:
                    pn = prev["blk"][1]
                    P.op("pool", [CP(u[:, :, 0:1], prev["u"][:, :, pn:pn + 1])], reads=[prev["b_u"]], writes=[b_u])
                if nxt is None:
                    P.op("pool", [MEMSET(u[:, :, ntok + 1:ntok + 2], 0.0)], writes=[b_u])
                else:
                    P.op("pool", [CP(u[:, :, ntok + 1:ntok + 2], nxt["u"][:, :, 1:2])], reads=[nxt["b_u"]], writes=[b_u])
                gb, b_gb = cur["gb"], cur["b_gb"]
                for c in range(8):
                    a_t, a_b = acc_t.next()
                    a = a_t[:, 0:ntok]
                    P.op("dve", [TS(a, u[:, c, 1:ntok + 1], cw1[:, c:c + 1], cbv[:, c:c + 1], ALU.mult, ALU.add)],
                         reads=[b_u, b_par], writes=[a_b])
                    P.op("dve", [STT(a, u[:, c, 0:ntok], cw0[:, c:c + 1], a, ALU.mult, ALU.add)], reads=[b_u, b_par, a_b], writes=[a_b])
                    P.op("dve", [STT(a, u[:, c, 2:ntok + 2], cw2[:, c:c + 1], a, ALU.mult, ALU.add)], reads=[b_u, b_par, a_b], writes=[a_b])
                    P.op("pool", [TT(vT[:, c, 0:ntok], a, gb[:, c, 0:ntok], ALU.mult)], reads=[a_b, b_gb], writes=[b_vT])
                xaT, b_xaT = cur["xaT"], cur["b_xaT"]
                g1 = V["g1"][s]
                for c in range(8):
                    bk, bb_ = ring.next()
                    P.op("pe", [MM(bk[:, 0:ntok], wco[:, k, c * 128:(c + 1) * 128], vT[:, k, 0:ntok], start=(k == 0), stop=(k == 7))
                                for k in range(8)], reads=[b_vT, wcob], writes=[bb_])
                    P.op("dve", [STT(xaT[:, c, 0:ntok], bk[:, 0:ntok], g1[:, c:c + 1], xaT[:, c, 0:ntok], ALU.mult, ALU.add)],
                         reads=[bb_, b_xaT, b_vec], writes=[b_xaT])
                tail(xaT, b_xaT, ntok, dst, tile0, xout)

            for blocks in seqs:
                st = [None, None]
                prev = None
                cur = None
                for i, blk in enumerate(blocks):
                    new = proj(blk)
                    if cur is not None:
                        convout(cur, prev, new)
                    prev, cur = cur, new
                convout(cur, prev, None)
            P.barrier()
            A.reset(m)

        ATT = {}

        def alloc_attn():
            ATT["KTa"] = A.alloc([4, NTOK], BF16)
            ATT["KTb"] = A.alloc([2, NTOK], BF16)
            ATT["VA"] = A.alloc([NKT, VA_W], BF16)
            ATT["b_KT"] = [Buf(f"KT{t}") for t in range(NKT)]
            ATT["b_ones"] = Buf("vaones")
            VA_ = ATT["VA"]
            P.op("pool", [MEMSET(VA_[:, :, 0:516].rearrange("p t (h w) -> p t h w", w=129)[:, :, :, 128:129], 1.0),
                          MEMSET(VA_[:, :, 516:646].rearrange("p t (g w) -> p t g w", w=65)[:, :, :, 64:65], 1.0)],
                 writes=[ATT["b_ones"]])

        def rope_ops(src, nvec, rc, rs, t1, t2):
            s3 = src.rearrange("p (v d) -> p v d", v=nvec)
            t13 = t1.rearrange("p (v d) -> p v d", v=nvec)
            rcb = rc.unsqueeze(1).broadcast_to([128, nvec, 64])
            s5 = src.rearrange("p (v a h f) -> p v a h f", v=nvec, a=2, h=2, f=16)
            t25 = t2.rearrange("p (v a h f) -> p v a h f", v=nvec, a=2, h=2, f=16)
            rs4 = rs.rearrange("p (a h f) -> p a h f", a=2, h=2, f=16)
            fns = [TT(t13, s3, rcb, ALU.mult)]
            for h in range(2):
                in1 = rs4[:, :, h, :].unsqueeze(1).broadcast_to([128, nvec, 2, 16])
                fns.append(TT(t25[:, :, :, h, :], s5[:, :, :, 1 - h, :], in1, ALU.mult))
            return fns

        def phase_kv(l, src, blocks):
            e = l // 2
            KTa, KTb, VA, b_KT, b_ones = ATT["KTa"], ATT["KTb"], ATT["VA"], ATT["b_KT"], ATT["b_ones"]
            m = A.mark()
            NT = 512
            wkv = A.alloc([8, 1280], BF16); wkvb = [Buf("wkv0"), Buf("wkv1"), Buf("wkv2")]
            hT = A.alloc([8, NT], BF16); b_hT = Buf("hT")
            xin = [(A.alloc([D]), Buf(f"xin{i}"), i) for i in range(4)]
            ropes = Ring([(A.alloc([64]), A.alloc([64]), Buf(f"rope{i}"), i) for i in range(2)])
            gk = A.alloc([128]); b_g = Buf("gk")
            t1r = Ring([(A.alloc([512]), A.alloc([512]), Buf(f"t12_{i}")) for i in range(2)])
            krr = Ring([(A.alloc([512], BF16), Buf(f"kr{i}")) for i in range(2)])
            sm = Ring([dict(sq=A.alloc([128]), ss=A.alloc([2]), rstd=A.alloc([2]), xg=A.alloc([128]), t1=A.alloc([128]),
                            t2=A.alloc([128]), kbd=A.alloc([256], BF16), buf=Buf(f"sm{i}")) for i in range(2)])
            ring = Ring([BK[i] for i in (0, 1, 2, 3, 4, 5, 6)])
            wv = w_in[e].rearrange("(k p) n -> p k n", p=128)
            P.op("pool", [DMA(wkv[:, :, 0:512], wv[:, :, 512:1024])], writes=[wkvb[0]], dma=("wkv", 0))
            P.op("pool", [DMA(wkv[:, :, 512:1024], wv[:, :, 1024:1536])], writes=[wkvb[1]], dma=("wkv", 1))
            P.op("pool", [DMA(wkv[:, :, 1024:1280], wv[:, :, 2048:2304])], writes=[wkvb[2]], dma=("wkv", 2))
            P.op("sp", [DMA(gk[:, 0:64], kng[e].partition_broadcast(128))], writes=[b_g], dma=("gk", 0))
            P.op("sp", [DMA(gk[:, 64:128], kng[e].partition_broadcast(128))], writes=[b_g], dma=("gk", 1))
            pb7 = banks[7].bitcast(BF16)
            for (tile0, ntok, s) in blocks:
                head(src, tile0, ntok, s, "1", hT, b_hT, None, None, xin, ring)
                for t in range(ntok // 128):
                    kt = tile0 + t
                    rc, rs, rb, ri = ropes.next()
                    P.op("sp", [DMA(rc, ropeC[kt * 128:(kt + 1) * 128, :])], writes=[rb], dma=("ropec", ri))
                    P.op("sp", [DMA(rs, ropeS[kt * 128:(kt + 1) * 128, :])], writes=[rb], dma=("ropes", ri))
                    lhs = [hT[:, k, t * 128:(t + 1) * 128] for k in range(8)]
                    zk, zkb = ring.next()
                    P.op("pe", [MM(zk, lhs[k], wkv[:, k, 0:512], start=(k == 0), stop=(k == 7)) for k in range(8)],
                         reads=[b_hT, wkvb[0]], writes=[zkb])
                    zv, zvb = ring.next()
                    P.op("pe", [MM(zv, lhs[k], wkv[:, k, 512:1024], start=(k == 0), stop=(k == 7)) for k in range(8)],
                         reads=[b_hT, wkvb[1]], writes=[zvb])
                    zb, zbb = ring.next()
                    P.op("pe", [MM(zb[:, 0:256], lhs[k], wkv[:, k, 1024:1280], start=(k == 0), stop=(k == 7)) for k in range(8)],
                         reads=[b_hT, wkvb[2]], writes=[zbb])
                    t1, t2, tb_ = t1r.next()
                    fns = rope_ops(zk, 8, rc, rs, t1, t2)
                    P.op("dve", fns, reads=[zkb, rb], writes=[tb_])
                    kr, krb = krr.next()
                    P.op("pool", [TT(kr, t1, t2, ALU.add)], reads=[tb_], writes=[krb])
                    P.op("pe", [TR(pb7[:, h * 128:(h + 1) * 128], kr[:, h * 128:(h + 1) * 128], identB) for h in range(4)],
                         reads=[krb, b_id], writes=[bbuf[7]])
                    P.op("act", [ACTV(KTa[:, :, kt * 128:(kt + 1) * 128], pb7[:, 0:512].rearrange("p (h t) -> p h t", h=4), AF.Copy)],
                         reads=[bbuf[7]], writes=[b_KT[kt]])
                    P.op("act", [ACTV(VA[:, kt, 0:516].rearrange("p (h w) -> p h w", w=129)[:, :, 0:128],
                                      zv.rearrange("p (h w) -> p h w", w=128), AF.Copy)], reads=[zvb, b_ones], writes=[b_KT[kt]])
                    q = sm.next()
                    P.op("act", [ACTV(q["sq"], zb[:, 0:128], AF.Square)], reads=[zbb], writes=[q["buf"]])
                    P.op("dve", [lambda e_, q=q: e_.tensor_reduce(out=q["ss"], in_=q["sq"].rearrange("p (h d) -> p h d", h=2),
                                                                  axis=AX.X, op=ALU.add)], reads=[q["buf"]], writes=[q["buf"]])
                    P.op("dve", [TS(q["ss"], q["ss"], 1.0 / 64, EPS, ALU.mult, ALU.add)], reads=[q["buf"]], writes=[q["buf"]])
                    P.op("pool", [TT(q["rstd"], q["ss"], neghalf[:, 0:2], ALU.pow)], reads=[q["buf"], b_const], writes=[q["buf"]])
                    P.op("dve", [TT(q["xg"], zb[:, 0:128], gk, ALU.mult)], reads=[zbb, b_g], writes=[q["buf"]])
                    P.op("dve", rope_ops(q["xg"], 2, rc, rs, q["t1"], q["t2"]), reads=[q["buf"], rb], writes=[q["buf"]])
                    P.op("pool", [TT(q["t1"], q["t1"], q["t2"], ALU.add)], reads=[q["buf"]], writes=[q["buf"]])
                    kb4 = q["kbd"].rearrange("p (g u d) -> p g u d", g=2, u=2, d=64)
                    rsb = q["rstd"].unsqueeze(2).broadcast_to([128, 2, 64])
                    t13 = q["t1"].rearrange("p (g d) -> p g d", g=2)
                    P.op("dve", [TT(kb4[:, :, 0, :], t13, rsb, ALU.mult), TT(kb4[:, :, 1, :], t13, rsb, ALU.mult)],
                         reads=[q["buf"]], writes=[q["buf"]])
                    P.op("pe", [TR(pb7[:, 512 + g * 128:512 + (g + 1) * 128], q["kbd"][:, g * 128:(g + 1) * 128], identB) for g in range(2)],
                         reads=[q["buf"], b_id], writes=[bbuf[7]])
                    P.op("act", [ACTV(KTb[:, :, kt * 128:(kt + 1) * 128], pb7[:, 512:768].rearrange("p (g t) -> p g t", g=2), AF.Copy)],
                         reads=[bbuf[7]], writes=[b_KT[kt]])
                    P.op("act", [ACTV(VA[:, kt, 516:646].rearrange("p (g w) -> p g w", w=65)[:, :, 0:64],
                                      zb[:, 128:256].rearrange("p (g w) -> p g w", w=64), AF.Copy)], reads=[zbb, b_ones], writes=[b_KT[kt]])
            P.barrier()
            A.reset(m)

        def phase_attn(l, src, dst, blocks):
            e = l // 2
            lam_init = 0.8 - 0.6 * float(np.exp(-0.3 * l))
            KTa, KTb, VA, b_KT, b_ones = ATT["KTa"], ATT["KTb"], ATT["VA"], ATT["b_KT"], ATT["b_ones"]
            m = A.mark()
            NT = 512
            wq = A.alloc([8, 1024], BF16); wqb = [Buf("wq0"), Buf("wq1")]
            wo = A.alloc([8, D], BF16); wob = Buf("wo")
            hT = A.alloc([8, NT], BF16); b_hT = Buf("hT")
            xaT = A.alloc([8, NT]); b_xaT = Buf("xaT")
            xin = [(A.alloc([D]), Buf(f"xin{i}"), i) for i in range(4)]
            xout = Ring([(xin[i][0], xin[i][1], i + 10) for i in range(2)])
            ropes = Ring([(A.alloc([64]), A.alloc([64]), Buf(f"rope{i}"), i) for i in range(2)])
            gq = A.alloc([64]); gsub = A.alloc([128]); lvt = A.alloc([256]); lam = A.alloc([8]); b_g = Buf("gq")
            scr = A.alloc([2048])
            t1 = scr[:, 0:512]; t2 = scr[:, 512:1024]; b_t12 = Buf("t12")
            sqq = scr[:, 1024:1536]; ssq = A.alloc([8]); rsq = A.alloc([8]); b_sq = Buf("sqq")
            qr = scr[:, 1536:2048].bitcast(BF16); b_qr = Buf("qr")
            QT = A.alloc([8, NT], BF16); b_QT = Buf("QT")
            PT = Ring([(A.alloc([NT], BF16), Buf(f"PT{i}")) for i in range(4)])
            otok = scr.bitcast(BF16).rearrange("p (q c) -> p q c", q=4); b_otok = [Buf(f"otok{i}") for i in range(4)]
            od = [A.alloc([4, 128]), A.alloc([4, 128])]; b_od = [Buf("od0"), Buf("od1")]
            dd = A.alloc([128]); junk = A.alloc([128]); b_dd = Buf("dd")
            rz_ring = Ring([(A.alloc([1]), Buf(f"rz{i}")) for i in range(4)])
            ssd_ring = Ring([(A.alloc([1]), A.alloc([1]), Buf(f"ssd{i}")) for i in range(2)])
            wv = w_in[e].rearrange("(k p) n -> p k n", p=128)
            P.op("pool", [DMA(wq[:, :, 0:512], wv[:, :, 0:512])], writes=[wqb[0]], dma=("wq", 0))
            P.op("pool", [DMA(wq[:, :, 512:1024], wv[:, :, 1536:2048])], writes=[wqb[1]], dma=("wq", 1))
            P.op("pool", [DMA(wo, w_out[e].rearrange("(k p) n -> p k n", p=128))], writes=[wob], dma=("wo", 0))
            P.op("sp", [DMA(gq, qng[e].partition_broadcast(128))], writes=[b_g], dma=("gq", 0))
            P.op("sp", [DMA(gsub, subg[e].partition_broadcast(128))], writes=[b_g], dma=("gq", 1))
            P.op("sp", [DMA(lvt, dlam[e].partition_broadcast(128))], writes=[b_g], dma=("gq", 2))
            P.op("dve", [lambda e_: e_.scalar_tensor_tensor(out=junk[:, 0:64], in0=lvt[:, 0:64], scalar=1.0, in1=lvt[:, 64:128],
                                                            op0=ALU.mult, op1=ALU.mult, accum_out=lam[:, 0:1])], reads=[b_g], writes=[b_dd])
            P.op("dve", [lambda e_: e_.scalar_tensor_tensor(out=junk[:, 0:64], in0=lvt[:, 128:192], scalar=1.0, in1=lvt[:, 192:256],
                                                            op0=ALU.mult, op1=ALU.mult, accum_out=lam[:, 1:2])], reads=[b_g, b_dd], writes=[b_dd])
            P.op("act", [ACTV(lam[:, 3:5], lam[:, 0:2], AF.Exp)], reads=[b_dd], writes=[b_dd])
            P.op("dve", [STT(lam[:, 2:3], lam[:, 4:5], -lam_init, lam[:, 3:4], ALU.add, ALU.subtract)], reads=[b_dd], writes=[b_dd])
            P.op("dve", [TS(gsub, gsub, 1.0 - lam_init, None, ALU.mult)], reads=[b_g], writes=[b_g])
            nlam = lam[:, 2:3]
            def acc_region(s_, qs):
                if qs < 3:
                    b = 4 + s_
                    return banks[b][:, qs * 129:(qs + 1) * 129], bbuf[b]
                return banks[6][:, s_ * 129:(s_ + 1) * 129], bbuf[6]
            P.op("dve", [MEMSET(banks[4], 0.0), MEMSET(banks[5], 0.0), MEMSET(banks[6], 0.0)], writes=[bbuf[4], bbuf[5], bbuf[6]])
            pb7 = banks[7].bitcast(BF16)
            Sring = Ring([BK[0], BK[1]])
            ring01 = Ring([BK[0], BK[1]])
            for (tile0, ntok, s) in blocks:
                nqs = ntok // 128
                keytiles = [0, 1] if s == 1 else list(range(NKT))
                head(src, tile0, ntok, s, "1", hT, b_hT, xaT, b_xaT, xin, ring01)
                for t in range(nqs):
                    kt = tile0 + t
                    rc, rs, rb, ri = ropes.next()
                    P.op("sp", [DMA(rc, ropeC[kt * 128:(kt + 1) * 128, :])], writes=[rb], dma=("ropec", ri))
                    P.op("sp", [DMA(rs, ropeS[kt * 128:(kt + 1) * 128, :])], writes=[rb], dma=("ropes", ri))
                    lhs = [hT[:, k, t * 128:(t + 1) * 128] for k in range(8)]
                    za, zab = BK[2]
                    zq, zqb = BK[3]
                    P.op("pe", [MM(za, lhs[k], wq[:, k, 0:512], start=(k == 0), stop=(k == 7)) for k in range(8)],
                         reads=[b_hT, wqb[0]], writes=[zab])
                    P.op("pe", [MM(zq, lhs[k], wq[:, k, 512:1024], start=(k == 0), stop=(k == 7)) for k in range(8)],
                         reads=[b_hT, wqb[1]], writes=[zqb])
                    P.op("dve", rope_ops(za, 8, rc, rs, t1, t2), reads=[zab, rb], writes=[b_t12])
                    P.op("pool", [TT(qr[:, 0:512], t1, t2, ALU.add)], reads=[b_t12], writes=[b_qr])
                    P.op("act", [ACTV(sqq, zq, AF.Square)], reads=[zqb], writes=[b_sq])
                    P.op("dve", [lambda e_: e_.tensor_reduce(out=ssq, in_=sqq.rearrange("p (h d) -> p h d", h=8), axis=AX.X, op=ALU.add)],
                         reads=[b_sq], writes=[b_sq])
                    P.op("dve", [TS(ssq, ssq, 1.0 / 64, EPS, ALU.mult, ALU.add)], reads=[b_sq], writes=[b_sq])
                    P.op("pool", [TT(rsq, ssq, neghalf[:, 0:8], ALU.pow)], reads=[b_sq, b_const], writes=[b_sq])
                    gqb = gq.unsqueeze(1).broadcast_to([128, 8, 64])
                    P.op("dve", [TT(sqq.rearrange("p (h d) -> p h d", h=8), zq.rearrange("p (h d) -> p h d", h=8), gqb, ALU.mult)],
                         reads=[zqb, b_g, b_sq], writes=[b_sq])
                    P.op("dve", rope_ops(sqq, 8, rc, rs, t1, t2), reads=[b_sq, rb, b_t12], writes=[b_t12])
                    P.op("pool", [TT(t1, t1, t2, ALU.add)], reads=[b_t12], writes=[b_t12])
                    P.op("dve", [TT(qr[:, 512:1024].rearrange("p (h d) -> p h d", h=8), t1.rearrange("p (h d) -> p h d", h=8),
                                    rsq.unsqueeze(2).broadcast_to([128, 8, 64]), ALU.mult)], reads=[b_t12, b_sq], writes=[b_qr])
                    P.op("pe", [TR(pb7[:, i * 128:(i + 1) * 128], qr[:, i * 128:(i + 1) * 128], identB) for i in range(8)],
                         reads=[b_qr, b_id], writes=[bbuf[7]])
                    P.op("act", [ACTV(QT[:, :, t * 128:(t + 1) * 128], pb7.rearrange("p (i t) -> p i t", i=8), AF.Copy)],
                         reads=[bbuf[7]], writes=[b_QT])
                steps = []
                for i in range(8):
                    for a in range(2):
                        for kt in keytiles:
                            steps.append((i, a, kt))
                pend = None

                def emit_qk(step):
                    i, a, kt = step
                    KTx = KTa[64 * a:64 * a + 64, i, kt * 128:(kt + 1) * 128] if i < 4 else \
                        KTb[64 * a:64 * a + 64, (i - 4) // 2, kt * 128:(kt + 1) * 128]
                    sb, sbb = Sring.next()
                    P.op("pe", [MM(sb[:, 0:ntok], KTx, QT[64 * a:64 * a + 64, i, 0:ntok])], reads=[b_KT[kt], b_QT], writes=[sbb])
                    pt, ptb = PT.next()
                    P.op("act", [ACTV(pt[:, 0:ntok], sb[:, 0:ntok], AF.Exp, scale=0.125)], reads=[sbb], writes=[ptb])
                    return (step, pt, ptb)

                def emit_pv(item):
                    (i, a, kt), pt, ptb = item
                    mset = (2 * i + a) % 2
                    if i < 4:
                        voff, W = i * 129, 129
                    else:
                        voff, W = 516 + ((i - 4) // 2) * 65, 65
                    fns = []
                    wr = []
                    for qs in range(nqs):
                        reg, rb_ = acc_region(mset, qs)
                        fns.append(MM(reg[:, 0:W], pt[:, qs * 128:(qs + 1) * 128], VA[:, kt, voff:voff + W],
                                      start=False, stop=False, skip_group_check=True))
                        if rb_ not in wr:
                            wr.append(rb_)
                    P.op("pe", fns, reads=[ptb, b_KT[kt]], writes=wr)

                def emit_evac(i, a):
                    mset = (2 * i + a) % 2
                    for qs in reversed(range(nqs)):
                        reg, rb_ = acc_region(mset, qs)
                        rz, rzb = rz_ring.next()
                        if i < 4:
                            P.op("dve", [RECIP(rz, reg[:, 128:129])], reads=[rb_], writes=[rzb])
                            P.op("dve", [TS(od[a][:, qs, :], reg[:, 0:128], rz[:, 0:1], None, ALU.mult)], reads=[rb_, rzb], writes=[b_od[a]])
                            P.op("dve", [MEMSET(reg[:, 0:129], 0.0)], writes=[rb_])
                        else:
                            head_ = 2 * (i - 4) + a
                            P.op("dve", [RECIP(rz, reg[:, 64:65])], reads=[rb_], writes=[rzb])
                            P.op("dve", [TS(otok[:, qs, 512 + head_ * 64:512 + (head_ + 1) * 64], reg[:, 0:64], rz[:, 0:1], None, ALU.mult)],
                                 reads=[rb_, rzb], writes=[b_otok[qs]])
                            P.op("dve", [MEMSET(reg[:, 0:65], 0.0)], writes=[rb_])
                    if i < 4 and a == 1:
                        for qs in range(nqs):
                            ss_, rinv, sb_ = ssd_ring.next()
                            P.op("dve", [STT(dd, od[1][:, qs, :], nlam, od[0][:, qs, :], ALU.mult, ALU.add)],
                                 reads=[b_od[0], b_od[1]], writes=[b_dd])
                            P.op("dve", [lambda e_, ss_=ss_: e_.scalar_tensor_tensor(out=junk, in0=dd, scalar=1.0, in1=dd,
                                                                                      op0=ALU.mult, op1=ALU.mult, accum_out=ss_)],
                                 reads=[b_dd], writes=[sb_, b_dd])
                            P.op("dve", [TS(ss_, ss_, 1.0 / 128, EPS, ALU.mult, ALU.add)], reads=[sb_], writes=[sb_])
                            P.op("pool", [TT(rinv, ss_, neghalf[:, 0:1], ALU.pow)], reads=[sb_, b_const], writes=[sb_])
                            P.op("dve", [STT(otok[:, qs, i * 128:(i + 1) * 128], dd, rinv[:, 0:1], gsub, ALU.mult, ALU.mult)],
                                 reads=[b_dd, sb_, b_g], writes=[b_otok[qs]])

                for n, step in enumerate(steps):
                    item = emit_qk(step)
                    if pend is not None:
                        emit_pv(pend)
                        pi, pa, pkt = pend[0]
                        if pkt == keytiles[-1]:
                            emit_evac(pi, pa)
                    pend = item
                emit_pv(pend)
                emit_evac(pend[0][0], pend[0][1])
                oT = hT
                for qs in range(nqs):
                    P.op("pe", [TR(pb7[:, c * 128:(c + 1) * 128], otok[:, qs, c * 128:(c + 1) * 128], identB) for c in range(8)],
                         reads=[b_otok[qs], b_id], writes=[bbuf[7]])
                    P.op("act", [ACTV(oT[:, :, qs * 128:(qs + 1) * 128], pb7.rearrange("p (c t) -> p c t", c=8), AF.Copy)],
                         reads=[bbuf[7]], writes=[b_hT])
                g1 = V["g1"][s]
                for c in range(8):
                    bk, bb_ = ring01.next()
                    P.op("pe", [MM(bk[:, 0:ntok], wo[:, k, c * 128:(c + 1) * 128], oT[:, k, 0:ntok], start=(k == 0), stop=(k == 7))
                                for k in range(8)], reads=[b_hT, wob], writes=[bb_])
                    P.op("dve", [STT(xaT[:, c, 0:ntok], bk[:, 0:ntok], g1[:, c:c + 1], xaT[:, c, 0:ntok], ALU.mult, ALU.add)],
                         reads=[bb_, b_xaT, b_vec], writes=[b_xaT])
                tail(xaT, b_xaT, ntok, dst, tile0, xout)
            P.barrier()
            A.reset(m)

        layer_vectors(0)
        mA = A.mark()
        alloc_attn()
        phase_kv(0, S_in, CTXB + LAT512)
        if stop_after == "kv0":
            P.emit(); return nc
        phase_attn(0, S_in, S_A, CTXB + LAT512)
        A.reset(mA)
        if stop_after == "attn0":
            P.emit(); return nc
        phase_mlp(0, S_A, S_B, CTXB + LAT256, False)
        if stop_after == "l0":
            P.emit(); return nc
        layer_vectors(1)
        phase_conv(1, S_B, S_A, [CTXB, LAT256])
        if stop_after == "conv1":
            P.emit(); return nc
        phase_mlp(1, S_A, S_B, CTXB + LAT256, False)
        layer_vectors(2)
        mA = A.mark()
        alloc_attn()
        phase_kv(2, S_B, CTXB + LAT512)
        phase_attn(2, S_B, S_A, LAT512)
        A.reset(mA)
        phase_mlp(2, S_A, S_B, LAT256, False)
        layer_vectors(3)
        phase_conv(3, S_B, S_A, [LAT256])
        phase_mlp(3, S_A, S_out, LAT256, True)
        P.emit()
        print("arena high-water words:", A.hi, "of", NWORDS, "ops:", P.nops)
    return nc


def _rope_tables():
    f32 = np.float32
    rows = S // 64
    row = np.repeat(np.arange(rows, dtype=f32), 64)
    col = np.tile(np.arange(64, dtype=f32), rows)
    inv_freq = (f32(10000.0) ** (-(np.arange(16, dtype=f32) / f32(16)))).astype(f32)
    ang = np.stack([row, col], axis=-1)[:, :, None].astype(f32) * inv_freq
    c = np.cos(ang).astype(f32)
    s = np.sin(ang).astype(f32)
    C = np.stack([c, c], axis=2).reshape(S, 64)
    Sg = np.stack([-s, s], axis=2).reshape(S, 64)
    C = np.concatenate([np.ones((CTX, 64), f32), C], axis=0)
    Sg = np.concatenate([np.zeros((CTX, 64), f32), Sg], axis=0)
    return np.ascontiguousarray(C), np.ascontiguousarray(Sg)


def make_in_maps(inputs):
    f32 = np.float32
    g = {k: np.asarray(v) for k, v in inputs.items()}
    ropeC_, ropeS_ = _rope_tables()
    lnT = np.stack([g["ln_g"].reshape(4, 2, 8, 128), g["ln_b"].reshape(4, 2, 8, 128)], axis=2)
    lnT = np.ascontiguousarray(lnT.transpose(4, 0, 1, 2, 3).reshape(128, 128)).astype(f32)
    cv = np.concatenate([g["conv_w"], g["conv_b"][:, None, :]], axis=1)
    convT = np.ascontiguousarray(cv.reshape(2, 4, 8, 128).transpose(3, 0, 1, 2).reshape(128, 64)).astype(f32)
    shared = {
        "ada_w": g["ada_w"], "ada_b": g["ada_b"], "attn_w_in": g["attn_w_in"], "attn_w_out": g["attn_w_out"],
        "diff_lambda": np.ascontiguousarray(g["diff_lambda"].reshape(2, 256)), "diff_subln_g": g["diff_subln_g"],
        "q_norm_g": g["q_norm_g"], "k_norm_g": g["k_norm_g"], "conv_w_in": g["conv_w_in"], "conv_w_out": g["conv_w_out"],
        "convT": convT, "mlp_w1": g["mlp_w1"], "mlp_w2": g["mlp_w2"], "lnT": lnT, "ln_g": g["ln_g"], "ln_b": g["ln_b"],
        "ropeC": ropeC_, "ropeS": ropeS_, "ident": np.eye(128, dtype=f32),
    }
    shared = {k: np.ascontiguousarray(v, dtype=f32) for k, v in shared.items()}
    maps = []
    for b in range(8):
        d = dict(shared)
        d["xs"] = np.ascontiguousarray(np.concatenate([g["ctx"][b], g["x"][b]], axis=0), dtype=f32)
        cvec = np.concatenate([g["c"][b].reshape(8, 128).T, g["c_ctx"].reshape(8, 128).T], axis=1)
        d["cvec"] = np.ascontiguousarray(cvec, dtype=f32)
        maps.append(d)
    return maps


_NC_CACHE = {}


def kernel(**inputs):
    maps = make_in_maps(inputs)
    if "nc" not in _NC_CACHE:
        _NC_CACHE["nc"] = build()
    res = run_bass_kernel_spmd(_NC_CACHE["nc"], maps, core_ids=list(range(8)))
    out = np.stack([np.asarray(res.results[b]["out"], dtype=np.float32) for b in range(8)], axis=0)
    return out
```

```python
import numpy as np
from contextlib import ExitStack
import concourse.bass as bass
import concourse.mybir as mybir
from concourse.bass_utils import run_bass_kernel_spmd

F32 = mybir.dt.float32
BF16 = mybir.dt.bfloat16
AF = mybir.ActivationFunctionType
ALU = mybir.AluOpType
AX = mybir.AxisListType

ENGS = ("pe", "act", "dve", "pool", "sp")


class Buf:
    __slots__ = ("name", "w", "r", "excl")

    def __init__(self, name, excl=False):
        self.name = name
        self.w = None
        self.r = {}
        self.excl = excl


class Prog:
    def __init__(self, nc, es):
        self.nc = nc
        self.es = es
        self.q = {e: [] for e in ENGS}
        self.sem = {}
        self.cnt = {}
        self.seen = {e: {} for e in ENGS}
        for e in ENGS:
            self._key(e)
        self.nops = 0

    def _key(self, key):
        if key not in self.sem:
            name = "s_" + "_".join(str(x) for x in (key if isinstance(key, tuple) else (key,)))
            self.sem[key] = self.es.enter_context(self.nc.semaphore(name))
            self.cnt[key] = 0
        return key

    def op(self, eng, fns, reads=(), writes=(), dma=None):
        deps = {}

        def need(k, v):
            if deps.get(k, 0) < v:
                deps[k] = v

        for b in reads:
            if b.w is not None:
                need(*b.w)
            if b.excl:
                for k, v in b.r.items():
                    need(k, v)
        for b in writes:
            if b.w is not None:
                need(*b.w)
            for k, v in b.r.items():
                need(k, v)
        key = self._key(dma) if dma is not None else eng
        inc = 16 if dma is not None else 1
        self.cnt[key] += inc
        val = self.cnt[key]
        waits = []
        for k, v in deps.items():
            if k == eng and eng == "pe":
                continue
            if self.seen[eng].get(k, 0) >= v:
                continue
            self.seen[eng][k] = v
            waits.append((k, v))
        self.q[eng].append((waits, list(fns), key, inc))
        self.nops += len(fns)
        for b in reads:
            if b.excl:
                b.w = (key, val)
                b.r = {}
            else:
                if b.r.get(key, 0) < val:
                    b.r[key] = val
        for b in writes:
            b.w = (key, val)
            b.r = {}
        return (key, val)

    def barrier(self, engs=ENGS):
        for e in engs:
            waits = []
            for k, v in self.cnt.items():
                if v == 0:
                    continue
                if k == e:
                    continue
                if self.seen[e].get(k, 0) >= v:
                    continue
                self.seen[e][k] = v
                waits.append((k, v))
            if waits:
                self.q[e].append((waits, [], None, 0))

    def emit(self):
        nc = self.nc
        block = self.es.enter_context(nc.Block())
        sem = self.sem

        def replay(ename):
            def body(e):
                for waits, fns, key, inc in self.q[ename]:
                    for k, v in waits:
                        e.wait_ge(sem[k], v)
                    if not fns:
                        continue
                    for f in fns[:-1]:
                        f(e)
                    fns[-1](e).then_inc(sem[key], inc)
            return body

        block.tensor(replay("pe"))
        block.scalar(replay("act"))
        block.vector(replay("dve"))
        block.gpsimd(replay("pool"))
        block.sync(replay("sp"))


def MM(out, lhsT, rhs, start=True, stop=True, **kw):
    return lambda e: e.matmul(out, lhsT, rhs, start=start, stop=stop, **kw)


def TR(out, in_, ident):
    return lambda e: e.transpose(out, in_, ident)


def ACTV(out, in_, func, bias=None, scale=None, accum_out=None):
    kw = {}
    if bias is not None:
        kw["bias"] = bias
    if scale is not None:
        kw["scale"] = scale
    if accum_out is not None:
        kw["accum_out"] = accum_out
    return lambda e: e.activation(out=out, in_=in_, func=func, **kw)


def TS(out, in0, s1, s2=None, op0=ALU.mult, op1=None):
    if op1 is None:
        return lambda e: e.tensor_scalar(out=out, in0=in0, scalar1=s1, scalar2=None, op0=op0)
    return lambda e: e.tensor_scalar(out=out, in0=in0, scalar1=s1, scalar2=s2, op0=op0, op1=op1)


def TT(out, in0, in1, op):
    return lambda e: e.tensor_tensor(out=out, in0=in0, in1=in1, op=op)


def STT(out, in0, scalar, in1, op0, op1):
    return lambda e: e.scalar_tensor_tensor(out=out, in0=in0, scalar=scalar, in1=in1, op0=op0, op1=op1)


def CP(out, in_):
    return lambda e: e.tensor_copy(out=out, in_=in_)


def MEMSET(ap, v):
    return lambda e: e.memset(ap, v)


def DMA(out, in_):
    return lambda e: e.dma_start(out=out, in_=in_)


def RECIP(out, in_):
    return lambda e: e.reciprocal(out=out, in_=in_)


class Arena:
    def __init__(self, ap, nwords):
        self.ap = ap
        self.n = nwords
        self.off = 0
        self.hi = 0

    def mark(self):
        return self.off

    def reset(self, off):
        self.off = off

    def alloc(self, shape, dtype=F32):
        n = int(np.prod(shape))
        words = n if dtype == F32 else (n + 1) // 2
        words = (words + 15) // 16 * 16
        assert self.off + words <= self.n, f"SBUF arena overflow {self.off}+{words}>{self.n}"
        v = self.ap[:, self.off:self.off + words]
        self.off += words
        self.hi = max(self.hi, self.off)
        if dtype != F32:
            v = v.bitcast(dtype)
        v = v[:, 0:n]
        if len(shape) == 1:
            return v
        names = " ".join(f"d{i}" for i in range(len(shape)))
        kw = {f"d{i}": int(s) for i, s in enumerate(shape)}
        return v.rearrange(f"p ({names}) -> p {names}", **kw)


D = 1024
DC = 8
S = 4096
CTX = 256
NTOK = S + CTX
NKT = NTOK // 128
H = 4096
HC = 32
DEPTH = 4
ALPHA = float((2 * DEPTH) ** 0.25)
EPS = 1e-6
VA_W = 4 * 129 + 2 * 65
NWORDS = 53000


class Ring:
    def __init__(self, items):
        self.items = items
        self.i = 0

    def next(self):
        it = self.items[self.i % len(self.items)]
        self.i += 1
        return it


class Stream:
    def __init__(self, ap, name, row_off=0):
        self.ap = ap
        self.bufs = [Buf(f"{name}{t}") for t in range(NKT)]
        self.row_off = row_off

    def rows(self, tile):
        r = tile * 128 - self.row_off
        return self.ap[r:r + 128, :]


def build(stop_after=None, debug=False):
    nc = bass.Bass("TRN2", target_bir_lowering=False)

    def din(name, shape, dt=F32):
        return nc.dram_tensor(name, list(shape), dt, kind="ExternalInput").ap()

    xs = din("xs", [NTOK, D])
    cvec = din("cvec", [128, 16])
    ada_w = din("ada_w", [4, D, 6 * D])
    ada_b = din("ada_b", [4, 6 * D])
    w_in = din("attn_w_in", [2, D, 2304])
    w_out = din("attn_w_out", [2, D, D])
    dlam = din("diff_lambda", [2, 256])
    subg = din("diff_subln_g", [2, 128])
    qng = din("q_norm_g", [2, 64])
    kng = din("k_norm_g", [2, 64])
    cw_in = din("conv_w_in", [2, D, 3 * D])
    cw_out = din("conv_w_out", [2, D, D])
    convT_d = din("convT", [128, 64])
    w1_d = din("mlp_w1", [4, D, H])
    w2_d = din("mlp_w2", [4, H, D])
    lnT_d = din("lnT", [128, 128])
    lng_d = din("ln_g", [4, 2, D])
    lnb_d = din("ln_b", [4, 2, D])
    ropeC = din("ropeC", [NTOK, 64])
    ropeS = din("ropeS", [NTOK, 64])
    ident_d = din("ident", [128, 128])
    out_d = nc.dram_tensor("out", [S, D], F32, kind="ExternalOutput").ap()
    skind = "ExternalOutput" if debug else "Internal"
    scrA = nc.dram_tensor("scrA", [NTOK, D], F32, kind=skind).ap()
    scrB = nc.dram_tensor("scrB", [NTOK, D], F32, kind=skind).ap()

    es = ExitStack()
    with es:
        arena_t = es.enter_context(nc.sbuf_tensor("arena", [128, NWORDS], F32))
        ps = es.enter_context(nc.psum_tensor("ps", [128, 4096], F32))
        es.enter_context(nc.allow_low_precision("bf16 matmul operands, fp32 accumulation"))
        A = Arena(arena_t, NWORDS)
        P = Prog(nc, es)
        banks = [ps[:, b * 512:(b + 1) * 512] for b in range(8)]
        bbuf = [Buf(f"bank{b}", excl=True) for b in range(8)]
        BK = [(banks[b], bbuf[b]) for b in range(8)]
        pair_ap = ps[:, 2 * 512:4 * 512]
        pair_bufs = [bbuf[2], bbuf[3]]

        S_in = Stream(xs, "xs")
        S_A = Stream(scrA, "sa")
        S_B = Stream(scrB, "sb")
        S_out = Stream(out_d, "so", row_off=CTX)

        identF = A.alloc([128]); identB = A.alloc([128], BF16); b_id = Buf("ident")
        cs = A.alloc([16]); csA = A.alloc([8, 2]); b_cs = Buf("cs")
        modT = A.alloc([4, 48, 2]); b_mod = Buf("modT")
        lnT = A.alloc([4, 2, 2, 8]); convT = A.alloc([2, 4, 8]); b_par = Buf("params")
        neghalf = A.alloc([16]); ones8 = A.alloc([8]); zeros8 = A.alloc([8]); b_const = Buf("const")
        V = {}
        for nm in ("A1", "B1", "g1", "A2", "B2", "g2"):
            V[nm] = [A.alloc([8]), A.alloc([8])]
        for nm in ("XA1", "XB1", "XA2", "XB2", "tmp"):
            V[nm] = A.alloc([8])
        b_vec = Buf("vecs")
        stat_ring = Ring([dict(st=A.alloc([2, 6]), mv=A.alloc([2]), ve=A.alloc([1]), rstd=A.alloc([1]),
                               nb=A.alloc([1]), buf=Buf(f"stat{i}")) for i in range(4)])

        P.op("sp", [DMA(identF, ident_d)], writes=[b_id], dma=("c", 0))
        P.op("pool", [DMA(identB, ident_d)], writes=[b_id], dma=("c", 1))
        P.op("sp", [DMA(cs, cvec)], writes=[b_cs], dma=("c", 2))
        P.op("sp", [DMA(lnT.rearrange("p a b c d -> p (a b c d)"), lnT_d)], writes=[b_par], dma=("c", 3))
        P.op("sp", [DMA(convT.rearrange("p a b c -> p (a b c)"), convT_d)], writes=[b_par], dma=("c", 4))
        P.op("pool", [MEMSET(neghalf, -0.5), MEMSET(ones8, 1.0), MEMSET(zeros8, 0.0)], writes=[b_const])
        P.op("act", [ACTV(csA[:, :, 0], cs[:, 0:8], AF.Silu), ACTV(csA[:, :, 1], cs[:, 8:16], AF.Silu)],
             reads=[b_cs], writes=[b_cs])

        m0 = A.mark()
        adab2 = A.alloc([6 * D]); b_adab = Buf("adab")
        modrow = A.alloc([6 * D]); b_modrow = Buf("modrow")
        wring = Ring([(A.alloc([8, 512]), Buf(f"adaw{i}"), i) for i in range(3)])
        for l in range(DEPTH):
            for r in range(2):
                P.op("sp", [DMA(adab2[r:r + 1, :], ada_b[l:l + 1, :])], writes=[b_adab], dma=("adab", r))
            for nb in range(12):
                wt, wb, wi = wring.next()
                P.op("sp", [DMA(wt, ada_w[l].rearrange("(k p) n -> p k n", p=128)[:, :, nb * 512:(nb + 1) * 512])],
                     writes=[wb], dma=("adaw", wi))
                bk, bb_ = BK[nb % 2]
                P.op("pe", [MM(bk[0:2, :], csA[:, k, :], wt[:, k, :], start=(k == 0), stop=(k == 7)) for k in range(8)],
                     reads=[wb, b_cs], writes=[bb_])
                P.op("dve", [TT(modrow[0:2, nb * 512:(nb + 1) * 512], bk[0:2, :], adab2[0:2, nb * 512:(nb + 1) * 512], ALU.add)],
                     reads=[bb_, b_adab], writes=[b_modrow])
            bk, bb_ = BK[4]
            P.op("pe", [TR(bk[:, 2 * i:2 * i + 2], modrow[0:2, i * 128:(i + 1) * 128], identF[0:2, 0:2]) for i in range(48)],
                 reads=[b_modrow, b_id], writes=[bb_])
            P.op("dve", [CP(modT[:, l].rearrange("p a b -> p (a b)"), bk[:, 0:96])], reads=[bb_], writes=[b_mod])
        P.barrier()
        A.reset(m0)
        if stop_after == "prologue":
            P.emit()
            return nc

        def mvec(l, j, s):
            return modT[:, l, j * 8:(j + 1) * 8, s]

        def layer_vectors(l):
            if l == 0:
                Gp, Bp = ones8, zeros8
            else:
                Gp, Bp = lnT[:, l - 1, 1, 0, :], lnT[:, l - 1, 1, 1, :]
            G2, B2 = lnT[:, l, 0, 0, :], lnT[:, l, 0, 1, :]
            ops = []
            for s in range(2):
                sh1, sc1, g1, sh2, sc2, g2 = [mvec(l, j, s) for j in range(6)]
                ops += [TS(V["tmp"], sc1, 1.0, None, ALU.add),
                        TT(V["A1"][s], V["tmp"], Gp, ALU.mult),
                        TT(V["B1"][s], V["tmp"], Bp, ALU.mult),
                        TT(V["B1"][s], V["B1"][s], sh1, ALU.add),
                        CP(V["g1"][s], g1),
                        TS(V["tmp"], sc2, 1.0, None, ALU.add),
                        TT(V["A2"][s], V["tmp"], G2, ALU.mult),
                        TT(V["B2"][s], V["tmp"], B2, ALU.mult),
                        TT(V["B2"][s], V["B2"][s], sh2, ALU.add),
                        CP(V["g2"][s], g2)]
            ops += [TS(V["XA1"], Gp, ALPHA, None, ALU.mult), TS(V["XB1"], Bp, ALPHA, None, ALU.mult),
                    TS(V["XA2"], G2, ALPHA, None, ALU.mult), TS(V["XB2"], B2, ALPHA, None, ALU.mult)]
            for o in ops:
                P.op("dve", [o], reads=[b_mod, b_par, b_const, b_vec], writes=[b_vec])

        def head_load(src, tile0, ntok, xin):
            for t in range(ntok // 128):
                xt, xb, xi = xin[t]
                P.op("sp", [DMA(xt, src.rows(tile0 + t))], reads=[src.bufs[tile0 + t]], writes=[xb], dma=("xin", xi))

        def head_compute(ntok, s, which, hT, b_hT, xaT, b_xaT, xin, ring):
            nt = ntok // 128
            Av, Bv = V["A" + which][s], V["B" + which][s]
            XAv, XBv = V["XA" + which], V["XB" + which]
            for c in range(8):
                bk, bb_ = ring.next()
                P.op("pe", [TR(bk[:, t * 128:(t + 1) * 128], xin[t][0][:, c * 128:(c + 1) * 128], identF) for t in range(nt)],
                     reads=[xin[t][1] for t in range(nt)] + [b_id], writes=[bb_])
                P.op("act", [ACTV(hT[:, c, 0:ntok], bk[:, 0:ntok], AF.Identity, bias=Bv[:, c:c + 1], scale=Av[:, c:c + 1])],
                     reads=[bb_, b_vec], writes=[b_hT])
                if xaT is not None:
                    P.op("dve", [TS(xaT[:, c, 0:ntok], bk[:, 0:ntok], XAv[:, c:c + 1], XBv[:, c:c + 1], ALU.mult, ALU.add)],
                         reads=[bb_, b_vec], writes=[b_xaT])

        def head(src, tile0, ntok, s, which, hT, b_hT, xaT, b_xaT, xin, ring):
            head_load(src, tile0, ntok, xin)
            head_compute(ntok, s, which, hT, b_hT, xaT, b_xaT, xin, ring)

        PAIR1 = Ring([(pair_ap, pair_bufs)])

        def tail_tile(rT, b_rT, t, dst, tile0, xout, final_tabs=None, pairs=None):
            pr_ap, pr_bufs = (pairs or PAIR1).next()
            P.op("pe", [TR(pr_ap[:, c * 128:(c + 1) * 128], rT[:, c, t * 128:(t + 1) * 128], identF) for c in range(8)],
                 reads=[b_rT, b_id], writes=pr_bufs)
            st = stat_ring.next()
            P.op("dve", [lambda e, st=st: e.bn_stats(out=st["st"][:, 0, :], in_=pr_ap[:, 0:512]),
                         lambda e, st=st: e.bn_stats(out=st["st"][:, 1, :], in_=pr_ap[:, 512:1024])],
                 reads=pr_bufs, writes=[st["buf"]])
            P.op("dve", [lambda e, st=st: e.bn_aggr(out=st["mv"], in_=st["st"].rearrange("p a b -> p (a b)"))],
                 reads=[st["buf"]], writes=[st["buf"]])
            P.op("dve", [TS(st["ve"], st["mv"][:, 1:2], EPS, None, ALU.add)], reads=[st["buf"]], writes=[st["buf"]])
            P.op("pool", [TT(st["rstd"], st["ve"], neghalf[:, 0:1], ALU.pow)], reads=[st["buf"], b_const], writes=[st["buf"]])
            P.op("dve", [STT(st["nb"], st["mv"][:, 0:1], -1.0, st["rstd"], ALU.mult, ALU.mult)],
                 reads=[st["buf"]], writes=[st["buf"]])
            xo, xob, xoi = xout.next()
            P.op("act", [ACTV(xo, pr_ap, AF.Identity, bias=st["nb"][:, 0:1], scale=st["rstd"][:, 0:1])],
                 reads=pr_bufs + [st["buf"]], writes=[xob])
            if final_tabs is not None:
                gt, bt, b_tab = final_tabs
                P.op("pool", [TT(xo, xo, gt, ALU.mult)], reads=[xob, b_tab], writes=[xob])
                P.op("pool", [TT(xo, xo, bt, ALU.add)], reads=[xob, b_tab], writes=[xob])
                P.op("sp", [DMA(dst.rows(tile0 + t), xo)], reads=[xob], writes=[dst.bufs[tile0 + t]], dma=("xout", xoi))
            else:
                P.op("act", [DMA(dst.rows(tile0 + t), xo)], reads=[xob], writes=[dst.bufs[tile0 + t]], dma=("xout", xoi))

        def tail(rT, b_rT, ntok, dst, tile0, xout, final_tabs=None, pairs=None):
            for t in range(ntok // 128):
                tail_tile(rT, b_rT, t, dst, tile0, xout, final_tabs, pairs)

        def load_rows_bf16(dst, src_rows, kgroups, key, bufs):
            pass

        LAT512 = [(2 + 4 * b, 512, 0) for b in range(8)]
        LAT256 = [(2 + 2 * b, 256, 0) for b in range(16)]
        CTXB = [(0, 256, 1)]

        def phase_mlp(l, src, dst, blocks, final):
            m = A.mark()
            NT = 256
            w1 = A.alloc([8, H], BF16); w1b = [Buf(f"w1_{i}") for i in range(8)]
            w2 = A.alloc([HC, D], BF16); w2b = [Buf(f"w2_{i}") for i in range(8)]
            hid = A.alloc([HC, NT], BF16); hidb = [Buf(f"hid{j}") for j in range(HC)]
            hTs = [(A.alloc([8, NT], BF16), Buf(f"hT{i}")) for i in range(2)]
            xaTs = [(A.alloc([8, NT]), Buf(f"xaT{i}")) for i in range(2)]
            xin = [(A.alloc([D]), Buf(f"xin{i}"), i) for i in range(2)]
            xout = Ring([(A.alloc([D]), Buf(f"xout{i}"), i) for i in range(2)])
            sq = Ring([(A.alloc([NT]), Buf(f"sq{i}")) for i in range(3)])
            ring = Ring([BK[i] for i in (0, 1, 4, 5)])
            pairs = Ring([(ps[:, 1024:2048], [bbuf[2], bbuf[3]]), (ps[:, 3072:4096], [bbuf[6], bbuf[7]])])
            tabs = None
            if final:
                gt = A.alloc([D]); bt = A.alloc([D]); b_tab = Buf("ftab")
                P.op("sp", [DMA(gt, lng_d[l, 1].partition_broadcast(128))], writes=[b_tab], dma=("ftab", 0))
                P.op("sp", [DMA(bt, lnb_d[l, 1].partition_broadcast(128))], writes=[b_tab], dma=("ftab", 1))
                tabs = (gt, bt, b_tab)
            w1v = w1_d[l].rearrange("(k p) n -> p k n", p=128)
            w2v = w2_d[l].rearrange("(j p) n -> p j n", p=128)
            for nb in range(8):
                P.op("pool", [DMA(w1[:, :, nb * 512:(nb + 1) * 512], w1v[:, :, nb * 512:(nb + 1) * 512])],
                     writes=[w1b[nb]], dma=("w1", nb))
            for jg in range(8):
                P.op("pool", [DMA(w2[:, jg * 4:(jg + 1) * 4, :], w2v[:, jg * 4:(jg + 1) * 4, :])],
                     writes=[w2b[jg]], dma=("w2", jg))
            nb_ = len(blocks)
            t0_, n0_, s0_ = blocks[0]
            head_load(src, t0_, n0_, xin)
            head_compute(n0_, s0_, "2", hTs[0][0], hTs[0][1], xaTs[0][0], xaTs[0][1], xin, ring)
            pending = []
            for bi, (tile0, ntok, s) in enumerate(blocks):
                cur = bi % 2
                hT, b_hT = hTs[cur]
                xaT, b_xaT = xaTs[cur]
                if bi + 1 < nb_:
                    head_load(src, blocks[bi + 1][0], blocks[bi + 1][1], xin)
                for j in range(HC):
                    bk, bb_ = ring.next()
                    P.op("pe", [MM(bk[:, 0:ntok], w1[:, k, j * 128:(j + 1) * 128], hT[:, k, 0:ntok], start=(k == 0), stop=(k == 7))
                                for k in range(8)], reads=[b_hT, w1b[j // 4]], writes=[bb_])
                    sqt, sqb = sq.next()
                    P.op("act", [ACTV(sqt[:, 0:ntok], bk[:, 0:ntok], AF.Square)], reads=[bb_], writes=[sqb])
                    P.op("dve", [STT(hid[:, j, 0:ntok], bk[:, 0:ntok], 0.0, sqt[:, 0:ntok], ALU.is_gt, ALU.mult)],
                         reads=[bb_, sqb], writes=[hidb[j]])
                    if j in (1, 15) and pending:
                        pending.pop(0)()
                while pending:
                    pending.pop(0)()
                if bi + 1 < nb_:
                    nx = 1 - cur
                    head_compute(blocks[bi + 1][1], blocks[bi + 1][2], "2", hTs[nx][0], hTs[nx][1], xaTs[nx][0], xaTs[nx][1], xin, ring)
                g2 = V["g2"][s]
                for c in range(8):
                    bk, bb_ = ring.next()
                    P.op("pe", [MM(bk[:, 0:ntok], w2[:, j, c * 128:(c + 1) * 128], hid[:, j, 0:ntok], start=(j == 0), stop=(j == HC - 1))
                                for j in range(HC)], reads=hidb + w2b, writes=[bb_])
                    P.op("dve", [STT(xaT[:, c, 0:ntok], bk[:, 0:ntok], g2[:, c:c + 1], xaT[:, c, 0:ntok], ALU.mult, ALU.add)],
                         reads=[bb_, b_xaT, b_vec], writes=[b_xaT])
                pending = [(lambda t=t, xaT=xaT, b_xaT=b_xaT, tile0=tile0: tail_tile(xaT, b_xaT, t, dst, tile0, xout, tabs, pairs))
                           for t in range(ntok // 128)]
            while pending:
                pending.pop(0)()
            P.barrier()
            A.reset(m)

        def phase_conv(l, src, dst, seqs):
            o = l // 2
            m = A.mark()
            NT = 256
            wci = A.alloc([8, 3 * D], BF16); wcib = [Buf(f"wci{i}") for i in range(6)]
            wco = A.alloc([8, D], BF16); wcob = Buf("wco")
            hT = A.alloc([8, NT], BF16); b_hT = Buf("hT")
            xa_ring = Ring([(A.alloc([8, NT]), Buf(f"xaT{i}")) for i in range(3)])
            gb_ring = Ring([(A.alloc([8, NT]), Buf(f"gb{i}")) for i in range(3)])
            u_ring = Ring([(A.alloc([8, NT + 2]), Buf(f"u{i}")) for i in range(4)])
            gct = Ring([(A.alloc([NT]), Buf(f"gct{i}")) for i in range(2)])
            acc_t = Ring([(A.alloc([NT]), Buf(f"cacc{i}")) for i in range(2)])
            vT = A.alloc([8, NT], BF16); b_vT = Buf("vT")
            xin = [(A.alloc([D]), Buf(f"xin{i}"), i) for i in range(2)]
            xout = Ring([(A.alloc([D]), Buf(f"xout{i}"), i) for i in range(2)])
            ring = Ring([BK[i] for i in (0, 1, 4, 5)])
            pairs = Ring([(ps[:, 1024:2048], [bbuf[2], bbuf[3]]), (ps[:, 3072:4096], [bbuf[6], bbuf[7]])])
            wv = cw_in[o].rearrange("(k p) n -> p k n", p=128)
            for nb in (0, 2, 4, 1, 3, 5):
                P.op("pool", [DMA(wci[:, :, nb * 512:(nb + 1) * 512], wv[:, :, nb * 512:(nb + 1) * 512])],
                     writes=[wcib[nb]], dma=("wci", nb))
            P.op("pool", [DMA(wco, cw_out[o].rearrange("(k p) n -> p k n", p=128))], writes=[wcob], dma=("wco", 0))
            cw0, cw1, cw2, cbv = [convT[:, o, i, :] for i in range(4)]

            def proj(blk, fillers, preloaded):
                tile0, ntok, s = blk
                xaT, b_xaT = xa_ring.next()
                gb, b_gb = gb_ring.next()
                u, b_u = u_ring.next()
                if not preloaded:
                    head_load(src, tile0, ntok, xin)
                head_compute(ntok, s, "1", hT, b_hT, xaT, b_xaT, xin, ring)
                for c in range(8):
                    bk, bb_ = ring.next()
                    P.op("pe", [MM(bk[:, 0:ntok], wci[:, k, c * 128:(c + 1) * 128], hT[:, k, 0:ntok], start=(k == 0), stop=(k == 7))
                                for k in range(8)], reads=[b_hT, wcib[c // 4]], writes=[bb_])
                    P.op("act", [ACTV(gb[:, c, 0:ntok], bk[:, 0:ntok], AF.Copy)], reads=[bb_], writes=[b_gb])
                    bk, bb_ = ring.next()
                    cc = 8 + c
                    P.op("pe", [MM(bk[:, 0:ntok], wci[:, k, cc * 128:(cc + 1) * 128], hT[:, k, 0:ntok], start=(k == 0), stop=(k == 7))
                                for k in range(8)], reads=[b_hT, wcib[cc // 4]], writes=[bb_])
                    g_t, g_b = gct.next()
                    P.op("act", [ACTV(g_t[:, 0:ntok], bk[:, 0:ntok], AF.Copy)], reads=[bb_], writes=[g_b])
                    bk, bb_ = ring.next()
                    cc = 16 + c
                    P.op("pe", [MM(bk[:, 0:ntok], wci[:, k, cc * 128:(cc + 1) * 128], hT[:, k, 0:ntok], start=(k == 0), stop=(k == 7))
                                for k in range(8)], reads=[b_hT, wcib[cc // 4]], writes=[bb_])
                    P.op("dve", [TT(u[:, c, 1:ntok + 1], bk[:, 0:ntok], g_t[:, 0:ntok], ALU.mult)], reads=[bb_, g_b], writes=[b_u])
                    if fillers:
                        fillers[c]()
                return dict(blk=blk, xaT=xaT, b_xaT=b_xaT, gb=gb, b_gb=b_gb, u=u, b_u=b_u)

            def conv_fillers(cur, prev, nxt):
                tile0, ntok, s = cur["blk"]
                u, b_u = cur["u"], cur["b_u"]
                gb, b_gb = cur["gb"], cur["b_gb"]

                def halo():
                    if prev is None:
                        P.op("pool", [MEMSET(u[:, :, 0:1], 0.0)], writes=[b_u])
                    else:
                        pn = prev["blk"][1]
                        P.op("pool", [CP(u[:, :, 0:1], prev["u"][:, :, pn:pn + 1])], reads=[prev["b_u"]], writes=[b_u])
                    if nxt is None:
                        P.op("pool", [MEMSET(u[:, :, ntok + 1:ntok + 2], 0.0)], writes=[b_u])
                    else:
                        P.op("pool", [CP(u[:, :, ntok + 1:ntok + 2], nxt["u"][:, :, 1:2])], reads=[nxt["b_u"]], writes=[b_u])

                def chunk(c):
                    if c == 0:
                        halo()
                    a_t, a_b = acc_t.next()
                    a = a_t[:, 0:ntok]
                    P.op("dve", [TS(a, u[:, c, 1:ntok + 1], cw1[:, c:c + 1], cbv[:, c:c + 1], ALU.mult, ALU.add)],
                         reads=[b_u, b_par], writes=[a_b])
                    P.op("dve", [STT(a, u[:, c, 0:ntok], cw0[:, c:c + 1], a, ALU.mult, ALU.add)], reads=[b_u, b_par, a_b], writes=[a_b])
                    P.op("dve", [STT(a, u[:, c, 2:ntok + 2], cw2[:, c:c + 1], a, ALU.mult, ALU.add)], reads=[b_u, b_par, a_b], writes=[a_b])
                    P.op("pool", [TT(vT[:, c, 0:ntok], a, gb[:, c, 0:ntok], ALU.mult)], reads=[a_b, b_gb], writes=[b_vT])
                return [(lambda c=c: chunk(c)) for c in range(8)]

            def outproj_tail(cur):
                tile0, ntok, s = cur["blk"]
                xaT, b_xaT = cur["xaT"], cur["b_xaT"]
                g1 = V["g1"][s]
                for c in range(8):
                    bk, bb_ = ring.next()
                    P.op("pe", [MM(bk[:, 0:ntok], wco[:, k, c * 128:(c + 1) * 128], vT[:, k, 0:ntok], start=(k == 0), stop=(k == 7))
                                for k in range(8)], reads=[b_vT, wcob], writes=[bb_])
                    P.op("dve", [STT(xaT[:, c, 0:ntok], bk[:, 0:ntok], g1[:, c:c + 1], xaT[:, c, 0:ntok], ALU.mult, ALU.add)],
                         reads=[bb_, b_xaT, b_vec], writes=[b_xaT])
                tail(xaT, b_xaT, ntok, dst, tile0, xout, None, pairs)

            for blocks in seqs:
                n = len(blocks)
                states = []
                for i, blk in enumerate(blocks):
                    fill = None
                    if i >= 2:
                        fill = conv_fillers(states[i - 2], states[i - 3] if i >= 3 else None, states[i - 1])
                    states.append(proj(blk, fill, preloaded=(i > 0)))
                    if i + 1 < n:
                        head_load(src, blocks[i + 1][0], blocks[i + 1][1], xin)
                    if i >= 2:
                        outproj_tail(states[i - 2])
                for k in range(max(0, n - 2), n):
                    for f in conv_fillers(states[k], states[k - 1] if k >= 1 else None, states[k + 1] if k + 1 < n else None):
                        f()
                    outproj_tail(states[k])
            P.barrier()
            A.reset(m)

        ATT = {}

        def alloc_attn():
            ATT["KTa"] = A.alloc([4, NTOK], BF16)
            ATT["KTb"] = A.alloc([2, NTOK], BF16)
            ATT["VA"] = A.alloc([NKT, VA_W], BF16)
            ATT["b_KT"] = [Buf(f"KT{t}") for t in range(NKT)]
            ATT["b_ones"] = Buf("vaones")
            VA_ = ATT["VA"]
            P.op("pool", [MEMSET(VA_[:, :, 0:516].rearrange("p t (h w) -> p t h w", w=129)[:, :, :, 128:129], 1.0),
                          MEMSET(VA_[:, :, 516:646].rearrange("p t (g w) -> p t g w", w=65)[:, :, :, 64:65], 1.0)],
                 writes=[ATT["b_ones"]])

        def rope_ops(src, nvec, rc, rs, t1, t2):
            s3 = src.rearrange("p (v d) -> p v d", v=nvec)
            t13 = t1.rearrange("p (v d) -> p v d", v=nvec)
            rcb = rc.unsqueeze(1).broadcast_to([128, nvec, 64])
            s5 = src.rearrange("p (v a h f) -> p v a h f", v=nvec, a=2, h=2, f=16)
            t25 = t2.rearrange("p (v a h f) -> p v a h f", v=nvec, a=2, h=2, f=16)
            rs4 = rs.rearrange("p (a h f) -> p a h f", a=2, h=2, f=16)
            fns = [TT(t13, s3, rcb, ALU.mult)]
            for h in range(2):
                in1 = rs4[:, :, h, :].unsqueeze(1).broadcast_to([128, nvec, 2, 16])
                fns.append(TT(t25[:, :, :, h, :], s5[:, :, :, 1 - h, :], in1, ALU.mult))
            return fns

        def phase_kv(l, src, blocks):
            e = l // 2
            KTa, KTb, VA, b_KT, b_ones = ATT["KTa"], ATT["KTb"], ATT["VA"], ATT["b_KT"], ATT["b_ones"]
            m = A.mark()
            NT = 512
            wkv = A.alloc([8, 1280], BF16); wkvb = [Buf("wkv0"), Buf("wkv1"), Buf("wkv2")]
            hT = A.alloc([8, NT], BF16); b_hT = Buf("hT")
            xin = [(A.alloc([D]), Buf(f"xin{i}"), i) for i in range(4)]
            ropes = Ring([(A.alloc([64]), A.alloc([64]), Buf(f"rope{i}"), i) for i in range(2)])
            gk = A.alloc([128]); b_g = Buf("gk")
            t1r = Ring([(A.alloc([512]), A.alloc([512]), Buf(f"t12_{i}")) for i in range(2)])
            krr = Ring([(A.alloc([512], BF16), Buf(f"kr{i}")) for i in range(2)])
            sm = Ring([dict(sq=A.alloc([128]), ss=A.alloc([2]), rstd=A.alloc([2]), xg=A.alloc([128]), t1=A.alloc([128]),
                            t2=A.alloc([128]), kbd=A.alloc([256], BF16), buf=Buf(f"sm{i}")) for i in range(2)])
            ring = Ring([BK[i] for i in (0, 1, 2, 3, 4, 5, 6)])
            wv = w_in[e].rearrange("(k p) n -> p k n", p=128)
            P.op("pool", [DMA(wkv[:, :, 0:512], wv[:, :, 512:1024])], writes=[wkvb[0]], dma=("wkv", 0))
            P.op("pool", [DMA(wkv[:, :, 512:1024], wv[:, :, 1024:1536])], writes=[wkvb[1]], dma=("wkv", 1))
            P.op("pool", [DMA(wkv[:, :, 1024:1280], wv[:, :, 2048:2304])], writes=[wkvb[2]], dma=("wkv", 2))
            P.op("sp", [DMA(gk[:, 0:64], kng[e].partition_broadcast(128))], writes=[b_g], dma=("gk", 0))
            P.op("sp", [DMA(gk[:, 64:128], kng[e].partition_broadcast(128))], writes=[b_g], dma=("gk", 1))
            pb7 = banks[7].bitcast(BF16)
            for (tile0, ntok, s) in blocks:
                head(src, tile0, ntok, s, "1", hT, b_hT, None, None, xin, ring)
                for t in range(ntok // 128):
                    kt = tile0 + t
                    rc, rs, rb, ri = ropes.next()
                    P.op("sp", [DMA(rc, ropeC[kt * 128:(kt + 1) * 128, :])], writes=[rb], dma=("ropec", ri))
                    P.op("sp", [DMA(rs, ropeS[kt * 128:(kt + 1) * 128, :])], writes=[rb], dma=("ropes", ri))
                    lhs = [hT[:, k, t * 128:(t + 1) * 128] for k in range(8)]
                    zk, zkb = ring.next()
                    P.op("pe", [MM(zk, lhs[k], wkv[:, k, 0:512], start=(k == 0), stop=(k == 7)) for k in range(8)],
                         reads=[b_hT, wkvb[0]], writes=[zkb])
                    zv, zvb = ring.next()
                    P.op("pe", [MM(zv, lhs[k], wkv[:, k, 512:1024], start=(k == 0), stop=(k == 7)) for k in range(8)],
                         reads=[b_hT, wkvb[1]], writes=[zvb])
                    zb, zbb = ring.next()
                    P.op("pe", [MM(zb[:, 0:256], lhs[k], wkv[:, k, 1024:1280], start=(k == 0), stop=(k == 7)) for k in range(8)],
                         reads=[b_hT, wkvb[2]], writes=[zbb])
                    t1, t2, tb_ = t1r.next()
                    fns = rope_ops(zk, 8, rc, rs, t1, t2)
                    P.op("dve", fns, reads=[zkb, rb], writes=[tb_])
                    kr, krb = krr.next()
                    P.op("pool", [TT(kr, t1, t2, ALU.add)], reads=[tb_], writes=[krb])
                    P.op("pe", [TR(pb7[:, h * 128:(h + 1) * 128], kr[:, h * 128:(h + 1) * 128], identB) for h in range(4)],
                         reads=[krb, b_id], writes=[bbuf[7]])
                    P.op("act", [ACTV(KTa[:, :, kt * 128:(kt + 1) * 128], pb7[:, 0:512].rearrange("p (h t) -> p h t", h=4), AF.Copy)],
                         reads=[bbuf[7]], writes=[b_KT[kt]])
                    P.op("act", [ACTV(VA[:, kt, 0:516].rearrange("p (h w) -> p h w", w=129)[:, :, 0:128],
                                      zv.rearrange("p (h w) -> p h w", w=128), AF.Copy)], reads=[zvb, b_ones], writes=[b_KT[kt]])
                    q = sm.next()
                    P.op("act", [ACTV(q["sq"], zb[:, 0:128], AF.Square)], reads=[zbb], writes=[q["buf"]])
                    P.op("dve", [lambda e_, q=q: e_.tensor_reduce(out=q["ss"], in_=q["sq"].rearrange("p (h d) -> p h d", h=2),
                                                                  axis=AX.X, op=ALU.add)], reads=[q["buf"]], writes=[q["buf"]])
                    P.op("dve", [TS(q["ss"], q["ss"], 1.0 / 64, EPS, ALU.mult, ALU.add)], reads=[q["buf"]], writes=[q["buf"]])
                    P.op("pool", [TT(q["rstd"], q["ss"], neghalf[:, 0:2], ALU.pow)], reads=[q["buf"], b_const], writes=[q["buf"]])
                    P.op("dve", [TT(q["xg"], zb[:, 0:128], gk, ALU.mult)], reads=[zbb, b_g], writes=[q["buf"]])
                    P.op("dve", rope_ops(q["xg"], 2, rc, rs, q["t1"], q["t2"]), reads=[q["buf"], rb], writes=[q["buf"]])
                    P.op("pool", [TT(q["t1"], q["t1"], q["t2"], ALU.add)], reads=[q["buf"]], writes=[q["buf"]])
                    kb4 = q["kbd"].rearrange("p (g u d) -> p g u d", g=2, u=2, d=64)
                    rsb = q["rstd"].unsqueeze(2).broadcast_to([128, 2, 64])
                    t13 = q["t1"].rearrange("p (g d) -> p g d", g=2)
                    P.op("dve", [TT(kb4[:, :, 0, :], t13, rsb, ALU.mult), TT(kb4[:, :, 1, :], t13, rsb, ALU.mult)],
                         reads=[q["buf"]], writes=[q["buf"]])
                    P.op("pe", [TR(pb7[:, 512 + g * 128:512 + (g + 1) * 128], q["kbd"][:, g * 128:(g + 1) * 128], identB) for g in range(2)],
                         reads=[q["buf"], b_id], writes=[bbuf[7]])
                    P.op("act", [ACTV(KTb[:, :, kt * 128:(kt + 1) * 128], pb7[:, 512:768].rearrange("p (g t) -> p g t", g=2), AF.Copy)],
                         reads=[bbuf[7]], writes=[b_KT[kt]])
                    P.op("act", [ACTV(VA[:, kt, 516:646].rearrange("p (g w) -> p g w", w=65)[:, :, 0:64],
                                      zb[:, 128:256].rearrange("p (g w) -> p g w", w=64), AF.Copy)], reads=[zbb, b_ones], writes=[b_KT[kt]])
            P.barrier()
            A.reset(m)

        def phase_attn(l, src, dst, blocks):
            e = l // 2
            lam_init = 0.8 - 0.6 * float(np.exp(-0.3 * l))
            KTa, KTb, VA, b_KT, b_ones = ATT["KTa"], ATT["KTb"], ATT["VA"], ATT["b_KT"], ATT["b_ones"]
            m = A.mark()
            NT = 512
            wq = A.alloc([8, 1024], BF16); wqb = [Buf("wq0"), Buf("wq1")]
            wo = A.alloc([8, D], BF16); wob = Buf("wo")
            hT = A.alloc([8, NT], BF16); b_hT = Buf("hT")
            xaT = A.alloc([8, NT]); b_xaT = Buf("xaT")
            xin = [(A.alloc([D]), Buf(f"xin{i}"), i) for i in range(4)]
            xout = Ring([(xin[i][0], xin[i][1], i + 10) for i in range(2)])
            ropes = Ring([(A.alloc([64]), A.alloc([64]), Buf(f"rope{i}"), i) for i in range(2)])
            gq = A.alloc([64]); gsub = A.alloc([128]); lvt = A.alloc([256]); lam = A.alloc([8]); b_g = Buf("gq")
            scr = A.alloc([2048])
            t1 = scr[:, 0:512]; t2 = scr[:, 512:1024]; b_t12 = Buf("t12")
            sqq = scr[:, 1024:1536]; ssq = A.alloc([8]); rsq = A.alloc([8]); b_sq = Buf("sqq")
            qr = scr[:, 1536:2048].bitcast(BF16); b_qr = Buf("qr")
            QT = A.alloc([8, NT], BF16); b_QT = Buf("QT")
            PT = Ring([(A.alloc([NT], BF16), Buf(f"PT{i}")) for i in range(5)])
            otok = scr.bitcast(BF16).rearrange("p (q c) -> p q c", q=4); b_otok = [Buf(f"otok{i}") for i in range(4)]
            od = [A.alloc([4, 128]), A.alloc([4, 128])]; b_od = [Buf("od0"), Buf("od1")]
            dd = A.alloc([4, 128]); junk = A.alloc([128]); b_dd = Buf("dd")
            rz_ring = Ring([(A.alloc([1]), Buf(f"rz{i}")) for i in range(4)])
            ssd_ring = Ring([(A.alloc([4]), A.alloc([4]), Buf(f"ssd{i}")) for i in range(2)])
            wv = w_in[e].rearrange("(k p) n -> p k n", p=128)
            P.op("pool", [DMA(wq[:, :, 0:512], wv[:, :, 0:512])], writes=[wqb[0]], dma=("wq", 0))
            P.op("pool", [DMA(wq[:, :, 512:1024], wv[:, :, 1536:2048])], writes=[wqb[1]], dma=("wq", 1))
            P.op("pool", [DMA(wo, w_out[e].rearrange("(k p) n -> p k n", p=128))], writes=[wob], dma=("wo", 0))
            P.op("sp", [DMA(gq, qng[e].partition_broadcast(128))], writes=[b_g], dma=("gq", 0))
            P.op("sp", [DMA(gsub, subg[e].partition_broadcast(128))], writes=[b_g], dma=("gq", 1))
            P.op("sp", [DMA(lvt, dlam[e].partition_broadcast(128))], writes=[b_g], dma=("gq", 2))
            P.op("dve", [lambda e_: e_.scalar_tensor_tensor(out=junk[:, 0:64], in0=lvt[:, 0:64], scalar=1.0, in1=lvt[:, 64:128],
                                                            op0=ALU.mult, op1=ALU.mult, accum_out=lam[:, 0:1])], reads=[b_g], writes=[b_dd])
            P.op("dve", [lambda e_: e_.scalar_tensor_tensor(out=junk[:, 0:64], in0=lvt[:, 128:192], scalar=1.0, in1=lvt[:, 192:256],
                                                            op0=ALU.mult, op1=ALU.mult, accum_out=lam[:, 1:2])], reads=[b_g, b_dd], writes=[b_dd])
            P.op("act", [ACTV(lam[:, 3:5], lam[:, 0:2], AF.Exp)], reads=[b_dd], writes=[b_dd])
            P.op("dve", [STT(lam[:, 2:3], lam[:, 4:5], -lam_init, lam[:, 3:4], ALU.add, ALU.subtract)], reads=[b_dd], writes=[b_dd])
            P.op("dve", [TS(gsub, gsub, 1.0 - lam_init, None, ALU.mult)], reads=[b_g], writes=[b_g])
            nlam = lam[:, 2:3]
            def acc_region(s_, qs):
                if qs < 3:
                    b = 4 + s_
                    return banks[b][:, qs * 129:(qs + 1) * 129], bbuf[b]
                return banks[6][:, s_ * 129:(s_ + 1) * 129], bbuf[6]
            P.op("dve", [MEMSET(banks[4], 0.0), MEMSET(banks[5], 0.0), MEMSET(banks[6], 0.0)], writes=[bbuf[4], bbuf[5], bbuf[6]])
            pb7 = banks[7].bitcast(BF16)
            Sring = Ring([BK[0], BK[1], BK[2], BK[3], BK[7]])
            ring01 = Ring([BK[0], BK[1]])
            LOOK = 3
            for (tile0, ntok, s) in blocks:
                nqs = ntok // 128
                keytiles = [0, 1] if s == 1 else list(range(NKT))
                head(src, tile0, ntok, s, "1", hT, b_hT, xaT, b_xaT, xin, ring01)
                for t in range(nqs):
                    kt = tile0 + t
                    rc, rs, rb, ri = ropes.next()
                    P.op("sp", [DMA(rc, ropeC[kt * 128:(kt + 1) * 128, :])], writes=[rb], dma=("ropec", ri))
                    P.op("sp", [DMA(rs, ropeS[kt * 128:(kt + 1) * 128, :])], writes=[rb], dma=("ropes", ri))
                    lhs = [hT[:, k, t * 128:(t + 1) * 128] for k in range(8)]
                    za, zab = BK[2]
                    zq, zqb = BK[3]
                    P.op("pe", [MM(za, lhs[k], wq[:, k, 0:512], start=(k == 0), stop=(k == 7)) for k in range(8)],
                         reads=[b_hT, wqb[0]], writes=[zab])
                    P.op("pe", [MM(zq, lhs[k], wq[:, k, 512:1024], start=(k == 0), stop=(k == 7)) for k in range(8)],
                         reads=[b_hT, wqb[1]], writes=[zqb])
                    P.op("dve", rope_ops(za, 8, rc, rs, t1, t2), reads=[zab, rb], writes=[b_t12])
                    P.op("pool", [TT(qr[:, 0:512], t1, t2, ALU.add)], reads=[b_t12], writes=[b_qr])
                    P.op("act", [ACTV(sqq, zq, AF.Square)], reads=[zqb], writes=[b_sq])
                    P.op("dve", [lambda e_: e_.tensor_reduce(out=ssq, in_=sqq.rearrange("p (h d) -> p h d", h=8), axis=AX.X, op=ALU.add)],
                         reads=[b_sq], writes=[b_sq])
                    P.op("dve", [TS(ssq, ssq, 1.0 / 64, EPS, ALU.mult, ALU.add)], reads=[b_sq], writes=[b_sq])
                    P.op("pool", [TT(rsq, ssq, neghalf[:, 0:8], ALU.pow)], reads=[b_sq, b_const], writes=[b_sq])
                    gqb = gq.unsqueeze(1).broadcast_to([128, 8, 64])
                    P.op("dve", [TT(sqq.rearrange("p (h d) -> p h d", h=8), zq.rearrange("p (h d) -> p h d", h=8), gqb, ALU.mult)],
                         reads=[zqb, b_g, b_sq], writes=[b_sq])
                    P.op("dve", rope_ops(sqq, 8, rc, rs, t1, t2), reads=[b_sq, rb, b_t12], writes=[b_t12])
                    P.op("pool", [TT(t1, t1, t2, ALU.add)], reads=[b_t12], writes=[b_t12])
                    P.op("dve", [TT(qr[:, 512:1024].rearrange("p (h d) -> p h d", h=8), t1.rearrange("p (h d) -> p h d", h=8),
                                    rsq.unsqueeze(2).broadcast_to([128, 8, 64]), ALU.mult)], reads=[b_t12, b_sq], writes=[b_qr])
                    P.op("pe", [TR(pb7[:, i * 128:(i + 1) * 128], qr[:, i * 128:(i + 1) * 128], identB) for i in range(8)],
                         reads=[b_qr, b_id], writes=[bbuf[7]])
                    P.op("act", [ACTV(QT[:, :, t * 128:(t + 1) * 128], pb7.rearrange("p (i t) -> p i t", i=8), AF.Copy)],
                         reads=[bbuf[7]], writes=[b_QT])
                steps = []
                for i in range(8):
                    for a in range(2):
                        for kt in keytiles:
                            steps.append((i, a, kt))

                def emit_qk(step):
                    i, a, kt = step
                    KTx = KTa[64 * a:64 * a + 64, i, kt * 128:(kt + 1) * 128] if i < 4 else \
                        KTb[64 * a:64 * a + 64, (i - 4) // 2, kt * 128:(kt + 1) * 128]
                    sb, sbb = Sring.next()
                    P.op("pe", [MM(sb[:, 0:ntok], KTx, QT[64 * a:64 * a + 64, i, 0:ntok])], reads=[b_KT[kt], b_QT], writes=[sbb])
                    pt, ptb = PT.next()
                    P.op("act", [ACTV(pt[:, 0:ntok], sb[:, 0:ntok], AF.Exp, scale=0.125)], reads=[sbb], writes=[ptb])
                    return (step, pt, ptb)

                def emit_pv(item):
                    (i, a, kt), pt, ptb = item
                    mset = (2 * i + a) % 2
                    if i < 4:
                        voff, W = i * 129, 129
                    else:
                        voff, W = 516 + ((i - 4) // 2) * 65, 65
                    fns = []
                    wr = []
                    for qs in range(nqs):
                        reg, rb_ = acc_region(mset, qs)
                        fns.append(MM(reg[:, 0:W], pt[:, qs * 128:(qs + 1) * 128], VA[:, kt, voff:voff + W],
                                      start=False, stop=False, skip_group_check=True))
                        if rb_ not in wr:
                            wr.append(rb_)
                    P.op("pe", fns, reads=[ptb, b_KT[kt]], writes=wr)

                def emit_evac(i, a):
                    mset = (2 * i + a) % 2
                    for qs in reversed(range(nqs)):
                        reg, rb_ = acc_region(mset, qs)
                        rz, rzb = rz_ring.next()
                        if i < 4:
                            P.op("dve", [RECIP(rz, reg[:, 128:129])], reads=[rb_], writes=[rzb])
                            P.op("dve", [TS(od[a][:, qs, :], reg[:, 0:128], rz[:, 0:1], None, ALU.mult)], reads=[rb_, rzb], writes=[b_od[a]])
                            P.op("dve", [MEMSET(reg[:, 0:129], 0.0)], writes=[rb_])
                        else:
                            head_ = 2 * (i - 4) + a
                            P.op("dve", [RECIP(rz, reg[:, 64:65])], reads=[rb_], writes=[rzb])
                            P.op("dve", [TS(otok[:, qs, 512 + head_ * 64:512 + (head_ + 1) * 64], reg[:, 0:64], rz[:, 0:1], None, ALU.mult)],
                                 reads=[rb_, rzb], writes=[b_otok[qs]])
                            P.op("dve", [MEMSET(reg[:, 0:65], 0.0)], writes=[rb_])
                    if i < 4 and a == 1:
                        ss_, rinv, sb_ = ssd_ring.next()
                        for qs in range(nqs):
                            P.op("dve", [STT(dd[:, qs, :], od[1][:, qs, :], nlam, od[0][:, qs, :], ALU.mult, ALU.add)],
                                 reads=[b_od[0], b_od[1]], writes=[b_dd])
                        for qs in range(nqs):
                            P.op("dve", [lambda e_, ss_=ss_, qs=qs: e_.scalar_tensor_tensor(out=junk, in0=dd[:, qs, :], scalar=1.0, in1=dd[:, qs, :],
                                                                                      op0=ALU.mult, op1=ALU.mult, accum_out=ss_[:, qs:qs + 1])],
                                 reads=[b_dd], writes=[sb_, b_dd])
                        P.op("dve", [TS(ss_[:, 0:nqs], ss_[:, 0:nqs], 1.0 / 128, EPS, ALU.mult, ALU.add)], reads=[sb_], writes=[sb_])
                        P.op("pool", [TT(rinv[:, 0:nqs], ss_[:, 0:nqs], neghalf[:, 0:nqs], ALU.pow)], reads=[sb_, b_const], writes=[sb_])
                        for qs in range(nqs):
                            P.op("dve", [STT(otok[:, qs, i * 128:(i + 1) * 128], dd[:, qs, :], rinv[:, qs:qs + 1], gsub, ALU.mult, ALU.mult)],
                                 reads=[b_dd, sb_, b_g], writes=[b_otok[qs]])

                pend = []
                for n, step in enumerate(steps):
                    pend.append(emit_qk(step))
                    if len(pend) > LOOK:
                        it = pend.pop(0)
                        emit_pv(it)
                        if it[0][2] == keytiles[-1]:
                            emit_evac(it[0][0], it[0][1])
                while pend:
                    it = pend.pop(0)
                    emit_pv(it)
                    if it[0][2] == keytiles[-1]:
                        emit_evac(it[0][0], it[0][1])
                oT = hT
                for qs in range(nqs):
                    P.op("pe", [TR(pb7[:, c * 128:(c + 1) * 128], otok[:, qs, c * 128:(c + 1) * 128], identB) for c in range(8)],
                         reads=[b_otok[qs], b_id], writes=[bbuf[7]])
                    P.op("act", [ACTV(oT[:, :, qs * 128:(qs + 1) * 128], pb7.rearrange("p (c t) -> p c t", c=8), AF.Copy)],
                         reads=[bbuf[7]], writes=[b_hT])
                g1 = V["g1"][s]
                for c in range(8):
                    bk, bb_ = ring01.next()
                    P.op("pe", [MM(bk[:, 0:ntok], wo[:, k, c * 128:(c + 1) * 128], oT[:, k, 0:ntok], start=(k == 0), stop=(k == 7))
                                for k in range(8)], reads=[b_hT, wob], writes=[bb_])
                    P.op("dve", [STT(xaT[:, c, 0:ntok], bk[:, 0:ntok], g1[:, c:c + 1], xaT[:, c, 0:ntok], ALU.mult, ALU.add)],
                         reads=[bb_, b_xaT, b_vec], writes=[b_xaT])
                tail(xaT, b_xaT, ntok, dst, tile0, xout)
            P.barrier()
            A.reset(m)

        layer_vectors(0)
        mA = A.mark()
        alloc_attn()
        phase_kv(0, S_in, CTXB + LAT512)
        if stop_after == "kv0":
            P.emit(); return nc
        phase_attn(0, S_in, S_A, CTXB + LAT512)
        A.reset(mA)
        if stop_after == "attn0":
            P.emit(); return nc
        phase_mlp(0, S_A, S_B, CTXB + LAT256, False)
        if stop_after == "l0":
            P.emit(); return nc
        layer_vectors(1)
        phase_conv(1, S_B, S_A, [CTXB, LAT256])
        if stop_after == "conv1":
            P.emit(); return nc
        phase_mlp(1, S_A, S_B, CTXB + LAT256, False)
        layer_vectors(2)
        mA = A.mark()
        alloc_attn()
        phase_kv(2, S_B, CTXB + LAT512)
        phase_attn(2, S_B, S_A, LAT512)
        A.reset(mA)
        phase_mlp(2, S_A, S_B, LAT256, False)
        layer_vectors(3)
        phase_conv(3, S_B, S_A, [LAT256])
        phase_mlp(3, S_A, S_out, LAT256, True)
        P.emit()
        print("arena high-water words:", A.hi, "of", NWORDS, "ops:", P.nops)
    return nc


def _rope_tables():
    f32 = np.float32
    rows = S // 64
    row = np.repeat(np.arange(rows, dtype=f32), 64)
    col = np.tile(np.arange(64, dtype=f32), rows)
    inv_freq = (f32(10000.0) ** (-(np.arange(16, dtype=f32) / f32(16)))).astype(f32)
    ang = np.stack([row, col], axis=-1)[:, :, None].astype(f32) * inv_freq
    c = np.cos(ang).astype(f32)
    s = np.sin(ang).astype(f32)
    C = np.stack([c, c], axis=2).reshape(S, 64)
    Sg = np.stack([-s, s], axis=2).reshape(S, 64)
    C = np.concatenate([np.ones((CTX, 64), f32), C], axis=0)
    Sg = np.concatenate([np.zeros((CTX, 64), f32), Sg], axis=0)
    return np.ascontiguousarray(C), np.ascontiguousarray(Sg)


def make_in_maps(inputs):
    f32 = np.float32
    g = {k: np.asarray(v) for k, v in inputs.items()}
    ropeC_, ropeS_ = _rope_tables()
    lnT = np.stack([g["ln_g"].reshape(4, 2, 8, 128), g["ln_b"].reshape(4, 2, 8, 128)], axis=2)
    lnT = np.ascontiguousarray(lnT.transpose(4, 0, 1, 2, 3).reshape(128, 128)).astype(f32)
    cv = np.concatenate([g["conv_w"], g["conv_b"][:, None, :]], axis=1)
    convT = np.ascontiguousarray(cv.reshape(2, 4, 8, 128).transpose(3, 0, 1, 2).reshape(128, 64)).astype(f32)
    shared = {
        "ada_w": g["ada_w"], "ada_b": g["ada_b"], "attn_w_in": g["attn_w_in"], "attn_w_out": g["attn_w_out"],
        "diff_lambda": np.ascontiguousarray(g["diff_lambda"].reshape(2, 256)), "diff_subln_g": g["diff_subln_g"],
        "q_norm_g": g["q_norm_g"], "k_norm_g": g["k_norm_g"], "conv_w_in": g["conv_w_in"], "conv_w_out": g["conv_w_out"],
        "convT": convT, "mlp_w1": g["mlp_w1"], "mlp_w2": g["mlp_w2"], "lnT": lnT, "ln_g": g["ln_g"], "ln_b": g["ln_b"],
        "ropeC": ropeC_, "ropeS": ropeS_, "ident": np.eye(128, dtype=f32),
    }
    shared = {k: np.ascontiguousarray(v, dtype=f32) for k, v in shared.items()}
    maps = []
    for b in range(8):
        d = dict(shared)
        d["xs"] = np.ascontiguousarray(np.concatenate([g["ctx"][b], g["x"][b]], axis=0), dtype=f32)
        cvec = np.concatenate([g["c"][b].reshape(8, 128).T, g["c_ctx"].reshape(8, 128).T], axis=1)
        d["cvec"] = np.ascontiguousarray(cvec, dtype=f32)
        maps.append(d)
    return maps


_NC_CACHE = {}


def kernel(**inputs):
    maps = make_in_maps(inputs)
    if "nc" not in _NC_CACHE:
        _NC_CACHE["nc"] = build()
    res = run_bass_kernel_spmd(_NC_CACHE["nc"], maps, core_ids=list(range(8)))
    out = np.stack([np.asarray(res.results[b]["out"], dtype=np.float32) for b in range(8)], axis=0)
    return out
```

```python
import numpy as np
from contextlib import ExitStack
import concourse.bass as bass
import concourse.mybir as mybir
from concourse.bass_utils import run_bass_kernel_spmd

F32 = mybir.dt.float32
BF16 = mybir.dt.bfloat16
AF = mybir.ActivationFunctionType
ALU = mybir.AluOpType
AX = mybir.AxisListType

ENGS = ("pe", "act", "dve", "pool", "sp")


class Buf:
    __slots__ = ("name", "w", "r", "excl")

    def __init__(self, name, excl=False):
        self.name = name
        self.w = None
        self.r = {}
        self.excl = excl


class Prog:
    def __init__(self, nc, es):
        self.nc = nc
        self.es = es
        self.q = {e: [] for e in ENGS}
        self.sem = {}
        self.cnt = {}
        self.seen = {e: {} for e in ENGS}
        for e in ENGS:
            self._key(e)
        self.nops = 0

    def _key(self, key):
        if key not in self.sem:
            name = "s_" + "_".join(str(x) for x in (key if isinstance(key, tuple) else (key,)))
            self.sem[key] = self.es.enter_context(self.nc.semaphore(name))
            self.cnt[key] = 0
        return key

    def op(self, eng, fns, reads=(), writes=(), dma=None):
        deps = {}

        def need(k, v):
            if deps.get(k, 0) < v:
                deps[k] = v

        for b in reads:
            if b.w is not None:
                need(*b.w)
            if b.excl:
                for k, v in b.r.items():
                    need(k, v)
        for b in writes:
            if b.w is not None:
                need(*b.w)
            for k, v in b.r.items():
                need(k, v)
        key = self._key(dma) if dma is not None else eng
        inc = 16 if dma is not None else 1
        self.cnt[key] += inc
        val = self.cnt[key]
        waits = []
        for k, v in deps.items():
            if k == eng and eng == "pe":
                continue
            if self.seen[eng].get(k, 0) >= v:
                continue
            self.seen[eng][k] = v
            waits.append((k, v))
        self.q[eng].append((waits, list(fns), key, inc))
        self.nops += len(fns)
        for b in reads:
            if b.excl:
                b.w = (key, val)
                b.r = {}
            else:
                if b.r.get(key, 0) < val:
                    b.r[key] = val
        for b in writes:
            b.w = (key, val)
            b.r = {}
        return (key, val)

    def barrier(self, engs=ENGS):
        for e in engs:
            waits = []
            for k, v in self.cnt.items():
                if v == 0:
                    continue
                if k == e:
                    continue
                if self.seen[e].get(k, 0) >= v:
                    continue
                self.seen[e][k] = v
                waits.append((k, v))
            if waits:
                self.q[e].append((waits, [], None, 0))

    def emit(self):
        nc = self.nc
        block = self.es.enter_context(nc.Block())
        sem = self.sem

        def replay(ename):
            def body(e):
                for waits, fns, key, inc in self.q[ename]:
                    for k, v in waits:
                        e.wait_ge(sem[k], v)
                    if not fns:
                        continue
                    for f in fns[:-1]:
                        f(e)
                    fns[-1](e).then_inc(sem[key], inc)
            return body

        block.tensor(replay("pe"))
        block.scalar(replay("act"))
        block.vector(replay("dve"))
        block.gpsimd(replay("pool"))
        block.sync(replay("sp"))


def MM(out, lhsT, rhs, start=True, stop=True, **kw):
    return lambda e: e.matmul(out, lhsT, rhs, start=start, stop=stop, **kw)


def TR(out, in_, ident):
    return lambda e: e.transpose(out, in_, ident)


def ACTV(out, in_, func, bias=None, scale=None, accum_out=None):
    kw = {}
    if bias is not None:
        kw["bias"] = bias
    if scale is not None:
        kw["scale"] = scale
    if accum_out is not None:
        kw["accum_out"] = accum_out
    return lambda e: e.activation(out=out, in_=in_, func=func, **kw)


def TS(out, in0, s1, s2=None, op0=ALU.mult, op1=None):
    if op1 is None:
        return lambda e: e.tensor_scalar(out=out, in0=in0, scalar1=s1, scalar2=None, op0=op0)
    return lambda e: e.tensor_scalar(out=out, in0=in0, scalar1=s1, scalar2=s2, op0=op0, op1=op1)


def TT(out, in0, in1, op):
    return lambda e: e.tensor_tensor(out=out, in0=in0, in1=in1, op=op)


def STT(out, in0, scalar, in1, op0, op1):
    return lambda e: e.scalar_tensor_tensor(out=out, in0=in0, scalar=scalar, in1=in1, op0=op0, op1=op1)


def CP(out, in_):
    return lambda e: e.tensor_copy(out=out, in_=in_)


def MEMSET(ap, v):
    return lambda e: e.memset(ap, v)


def DMA(out, in_):
    return lambda e: e.dma_start(out=out, in_=in_)


def RECIP(out, in_):
    return lambda e: e.reciprocal(out=out, in_=in_)


class Arena:
    def __init__(self, ap, nwords):
        self.ap = ap
        self.n = nwords
        self.off = 0
        self.hi = 0

    def mark(self):
        return self.off

    def reset(self, off):
        self.off = off

    def alloc(self, shape, dtype=F32):
        n = int(np.prod(shape))
        words = n if dtype == F32 else (n + 1) // 2
        words = (words + 15) // 16 * 16
        assert self.off + words <= self.n, f"SBUF arena overflow {self.off}+{words}>{self.n}"
        v = self.ap[:, self.off:self.off + words]
        self.off += words
        self.hi = max(self.hi, self.off)
        if dtype != F32:
            v = v.bitcast(dtype)
        v = v[:, 0:n]
        if len(shape) == 1:
            return v
        names = " ".join(f"d{i}" for i in range(len(shape)))
        kw = {f"d{i}": int(s) for i, s in enumerate(shape)}
        return v.rearrange(f"p ({names}) -> p {names}", **kw)


D = 1024
DC = 8
S = 4096
CTX = 256
NTOK = S + CTX
NKT = NTOK // 128
H = 4096
HC = 32
DEPTH = 4
ALPHA = float((2 * DEPTH) ** 0.25)
EPS = 1e-6
VA_W = 4 * 129 + 2 * 65
NWORDS = 53000


class Ring:
    def __init__(self, items):
        self.items = items
        self.i = 0

    def next(self):
        it = self.items[self.i % len(self.items)]
        self.i += 1
        return it


class Stream:
    def __init__(self, ap, name, row_off=0):
        self.ap = ap
        self.bufs = [Buf(f"{name}{t}") for t in range(NKT)]
        self.row_off = row_off

    def rows(self, tile):
        r = tile * 128 - self.row_off
        return self.ap[r:r + 128, :]


def build(stop_after=None, debug=False):
    nc = bass.Bass("TRN2", target_bir_lowering=False)

    def din(name, shape, dt=F32):
        return nc.dram_tensor(name, list(shape), dt, kind="ExternalInput").ap()

    xs = din("xs", [NTOK, D])
    cvec = din("cvec", [128, 16])
    ada_w = din("ada_w", [4, D, 6 * D])
    ada_b = din("ada_b", [4, 6 * D])
    w_in = din("attn_w_in", [2, D, 2304])
    w_out = din("attn_w_out", [2, D, D])
    dlam = din("diff_lambda", [2, 256])
    subg = din("diff_subln_g", [2, 128])
    qng = din("q_norm_g", [2, 64])
    kng = din("k_norm_g", [2, 64])
    cw_in = din("conv_w_in", [2, D, 3 * D])
    cw_out = din("conv_w_out", [2, D, D])
    convT_d = din("convT", [128, 64])
    w1_d = din("mlp_w1", [4, D, H])
    w2_d = din("mlp_w2", [4, H, D])
    lnT_d = din("lnT", [128, 128])
    lng_d = din("ln_g", [4, 2, D])
    lnb_d = din("ln_b", [4, 2, D])
    ropeC = din("ropeC", [NTOK, 64])
    ropeS = din("ropeS", [NTOK, 64])
    ident_d = din("ident", [128, 128])
    out_d = nc.dram_tensor("out", [S, D], F32, kind="ExternalOutput").ap()
    skind = "ExternalOutput" if debug else "Internal"
    scrA = nc.dram_tensor("scrA", [NTOK, D], F32, kind=skind).ap()
    scrB = nc.dram_tensor("scrB", [NTOK, D], F32, kind=skind).ap()

    es = ExitStack()
    with es:
        arena_t = es.enter_context(nc.sbuf_tensor("arena", [128, NWORDS], F32))
        ps = es.enter_context(nc.psum_tensor("ps", [128, 4096], F32))
        es.enter_context(nc.allow_low_precision("bf16 matmul operands, fp32 accumulation"))
        A = Arena(arena_t, NWORDS)
        P = Prog(nc, es)
        banks = [ps[:, b * 512:(b + 1) * 512] for b in range(8)]
        bbuf = [Buf(f"bank{b}", excl=True) for b in range(8)]
        BK = [(banks[b], bbuf[b]) for b in range(8)]
        pair_ap = ps[:, 2 * 512:4 * 512]
        pair_bufs = [bbuf[2], bbuf[3]]

        S_in = Stream(xs, "xs")
        S_A = Stream(scrA, "sa")
        S_B = Stream(scrB, "sb")
        S_out = Stream(out_d, "so", row_off=CTX)

        identF = A.alloc([128]); identB = A.alloc([128], BF16); b_id = Buf("ident")
        cs = A.alloc([16]); csA = A.alloc([8, 2]); b_cs = Buf("cs")
        modT = A.alloc([4, 48, 2]); b_mod = Buf("modT")
        lnT = A.alloc([4, 2, 2, 8]); convT = A.alloc([2, 4, 8]); b_par = Buf("params")
        neghalf = A.alloc([16]); ones8 = A.alloc([8]); zeros8 = A.alloc([8]); b_const = Buf("const")
        V = {}
        for nm in ("A1", "B1", "g1", "A2", "B2", "g2"):
            V[nm] = [A.alloc([8]), A.alloc([8])]
        for nm in ("XA1", "XB1", "XA2", "XB2", "tmp"):
            V[nm] = A.alloc([8])
        b_vec = Buf("vecs")
        stat_ring = Ring([dict(st=A.alloc([2, 6]), mv=A.alloc([2]), ve=A.alloc([1]), rstd=A.alloc([1]),
                               nb=A.alloc([1]), buf=Buf(f"stat{i}")) for i in range(4)])

        P.op("sp", [DMA(identF, ident_d)], writes=[b_id], dma=("c", 0))
        P.op("pool", [DMA(identB, ident_d)], writes=[b_id], dma=("c", 1))
        P.op("sp", [DMA(cs, cvec)], writes=[b_cs], dma=("c", 2))
        P.op("sp", [DMA(lnT.rearrange("p a b c d -> p (a b c d)"), lnT_d)], writes=[b_par], dma=("c", 3))
        P.op("sp", [DMA(convT.rearrange("p a b c -> p (a b c)"), convT_d)], writes=[b_par], dma=("c", 4))
        P.op("pool", [MEMSET(neghalf, -0.5), MEMSET(ones8, 1.0), MEMSET(zeros8, 0.0)], writes=[b_const])
        P.op("act", [ACTV(csA[:, :, 0], cs[:, 0:8], AF.Silu), ACTV(csA[:, :, 1], cs[:, 8:16], AF.Silu)],
             reads=[b_cs], writes=[b_cs])

        m0 = A.mark()
        adab2 = A.alloc([6 * D]); b_adab = Buf("adab")
        modrow = A.alloc([6 * D]); b_modrow = Buf("modrow")
        wring = Ring([(A.alloc([8, 512]), Buf(f"adaw{i}"), i) for i in range(3)])
        for l in range(DEPTH):
            for r in range(2):
                P.op("sp", [DMA(adab2[r:r + 1, :], ada_b[l:l + 1, :])], writes=[b_adab], dma=("adab", r))
            for nb in range(12):
                wt, wb, wi = wring.next()
                P.op("sp", [DMA(wt, ada_w[l].rearrange("(k p) n -> p k n", p=128)[:, :, nb * 512:(nb + 1) * 512])],
                     writes=[wb], dma=("adaw", wi))
                bk, bb_ = BK[nb % 2]
                P.op("pe", [MM(bk[0:2, :], csA[:, k, :], wt[:, k, :], start=(k == 0), stop=(k == 7)) for k in range(8)],
                     reads=[wb, b_cs], writes=[bb_])
                P.op("dve", [TT(modrow[0:2, nb * 512:(nb + 1) * 512], bk[0:2, :], adab2[0:2, nb * 512:(nb + 1) * 512], ALU.add)],
                     reads=[bb_, b_adab], writes=[b_modrow])
            bk, bb_ = BK[4]
            P.op("pe", [TR(bk[:, 2 * i:2 * i + 2], modrow[0:2, i * 128:(i + 1) * 128], identF[0:2, 0:2]) for i in range(48)],
                 reads=[b_modrow, b_id], writes=[bb_])
            P.op("dve", [CP(modT[:, l].rearrange("p a b -> p (a b)"), bk[:, 0:96])], reads=[bb_], writes=[b_mod])
        P.barrier()
        A.reset(m0)
        if stop_after == "prologue":
            P.emit()
            return nc

        def mvec(l, j, s):
            return modT[:, l, j * 8:(j + 1) * 8, s]

        def layer_vectors(l):
            if l == 0:
                Gp, Bp = ones8, zeros8
            else:
                Gp, Bp = lnT[:, l - 1, 1, 0, :], lnT[:, l - 1, 1, 1, :]
            G2, B2 = lnT[:, l, 0, 0, :], lnT[:, l, 0, 1, :]
            ops = []
            for s in range(2):
                sh1, sc1, g1, sh2, sc2, g2 = [mvec(l, j, s) for j in range(6)]
                ops += [TS(V["tmp"], sc1, 1.0, None, ALU.add),
                        TT(V["A1"][s], V["tmp"], Gp, ALU.mult),
                        TT(V["B1"][s], V["tmp"], Bp, ALU.mult),
                        TT(V["B1"][s], V["B1"][s], sh1, ALU.add),
                        CP(V["g1"][s], g1),
                        TS(V["tmp"], sc2, 1.0, None, ALU.add),
                        TT(V["A2"][s], V["tmp"], G2, ALU.mult),
                        TT(V["B2"][s], V["tmp"], B2, ALU.mult),
                        TT(V["B2"][s], V["B2"][s], sh2, ALU.add),
                        CP(V["g2"][s], g2)]
            ops += [TS(V["XA1"], Gp, ALPHA, None, ALU.mult), TS(V["XB1"], Bp, ALPHA, None, ALU.mult),
                    TS(V["XA2"], G2, ALPHA, None, ALU.mult), TS(V["XB2"], B2, ALPHA, None, ALU.mult)]
            for o in ops:
                P.op("dve", [o], reads=[b_mod, b_par, b_const, b_vec], writes=[b_vec])

        def head_load(src, tile0, ntok, xin):
            for t in range(ntok // 128):
                xt, xb, xi = xin[t]
                P.op("sp", [DMA(xt, src.rows(tile0 + t))], reads=[src.bufs[tile0 + t]], writes=[xb], dma=("xin", xi))

        def head_compute(ntok, s, which, hT, b_hT, xaT, b_xaT, xin, ring):
            nt = ntok // 128
            Av, Bv = V["A" + which][s], V["B" + which][s]
            XAv, XBv = V["XA" + which], V["XB" + which]
            for c in range(8):
                bk, bb_ = ring.next()
                P.op("pe", [TR(bk[:, t * 128:(t + 1) * 128], xin[t][0][:, c * 128:(c + 1) * 128], identF) for t in range(nt)],
                     reads=[xin[t][1] for t in range(nt)] + [b_id], writes=[bb_])
                P.op("act", [ACTV(hT[:, c, 0:ntok], bk[:, 0:ntok], AF.Identity, bias=Bv[:, c:c + 1], scale=Av[:, c:c + 1])],
                     reads=[bb_, b_vec], writes=[b_hT])
                if xaT is not None:
                    P.op("dve", [TS(xaT[:, c, 0:ntok], bk[:, 0:ntok], XAv[:, c:c + 1], XBv[:, c:c + 1], ALU.mult, ALU.add)],
                         reads=[bb_, b_vec], writes=[b_xaT])

        def head(src, tile0, ntok, s, which, hT, b_hT, xaT, b_xaT, xin, ring):
            head_load(src, tile0, ntok, xin)
            head_compute(ntok, s, which, hT, b_hT, xaT, b_xaT, xin, ring)

        PAIR1 = Ring([(pair_ap, pair_bufs)])

        def tail_tile(rT, b_rT, t, dst, tile0, xout, final_tabs=None, pairs=None):
            pr_ap, pr_bufs = (pairs or PAIR1).next()
            P.op("pe", [TR(pr_ap[:, c * 128:(c + 1) * 128], rT[:, c, t * 128:(t + 1) * 128], identF) for c in range(8)],
                 reads=[b_rT, b_id], writes=pr_bufs)
            st = stat_ring.next()
            P.op("dve", [lambda e, st=st: e.bn_stats(out=st["st"][:, 0, :], in_=pr_ap[:, 0:512]),
                         lambda e, st=st: e.bn_stats(out=st["st"][:, 1, :], in_=pr_ap[:, 512:1024])],
                 reads=pr_bufs, writes=[st["buf"]])
            P.op("dve", [lambda e, st=st: e.bn_aggr(out=st["mv"], in_=st["st"].rearrange("p a b -> p (a b)"))],
                 reads=[st["buf"]], writes=[st["buf"]])
            P.op("dve", [TS(st["ve"], st["mv"][:, 1:2], EPS, None, ALU.add)], reads=[st["buf"]], writes=[st["buf"]])
            P.op("pool", [TT(st["rstd"], st["ve"], neghalf[:, 0:1], ALU.pow)], reads=[st["buf"], b_const], writes=[st["buf"]])
            P.op("dve", [STT(st["nb"], st["mv"][:, 0:1], -1.0, st["rstd"], ALU.mult, ALU.mult)],
                 reads=[st["buf"]], writes=[st["buf"]])
            xo, xob, xoi = xout.next()
            P.op("act", [ACTV(xo, pr_ap, AF.Identity, bias=st["nb"][:, 0:1], scale=st["rstd"][:, 0:1])],
                 reads=pr_bufs + [st["buf"]], writes=[xob])
            if final_tabs is not None:
                gt, bt, b_tab = final_tabs
                P.op("pool", [TT(xo, xo, gt, ALU.mult)], reads=[xob, b_tab], writes=[xob])
                P.op("pool", [TT(xo, xo, bt, ALU.add)], reads=[xob, b_tab], writes=[xob])
                P.op("sp", [DMA(dst.rows(tile0 + t), xo)], reads=[xob], writes=[dst.bufs[tile0 + t]], dma=("xout", xoi))
            else:
                P.op("act", [DMA(dst.rows(tile0 + t), xo)], reads=[xob], writes=[dst.bufs[tile0 + t]], dma=("xout", xoi))

        def tail(rT, b_rT, ntok, dst, tile0, xout, final_tabs=None, pairs=None):
            for t in range(ntok // 128):
                tail_tile(rT, b_rT, t, dst, tile0, xout, final_tabs, pairs)

        def load_rows_bf16(dst, src_rows, kgroups, key, bufs):
            pass

        LAT512 = [(2 + 4 * b, 512, 0) for b in range(8)]
        LAT256 = [(2 + 2 * b, 256, 0) for b in range(16)]
        CTXB = [(0, 256, 1)]

        def phase_mlp(l, src, dst, blocks, final):
            m = A.mark()
            NT = 256
            w1 = A.alloc([8, H], BF16); w1b = [Buf(f"w1_{i}") for i in range(8)]
            w2 = A.alloc([HC, D], BF16); w2b = [Buf(f"w2_{i}") for i in range(8)]
            hid = A.alloc([HC, NT], BF16); hidb = [Buf(f"hid{j}") for j in range(HC)]
            hTs = [(A.alloc([8, NT], BF16), Buf(f"hT{i}")) for i in range(2)]
            xaTs = [(A.alloc([8, NT]), Buf(f"xaT{i}")) for i in range(2)]
            xin = [(A.alloc([D]), Buf(f"xin{i}"), i) for i in range(2)]
            xout = Ring([(A.alloc([D]), Buf(f"xout{i}"), i) for i in range(2)])
            sq = Ring([(A.alloc([NT]), Buf(f"sq{i}")) for i in range(3)])
            ring = Ring([BK[i] for i in (0, 1, 4, 5)])
            pairs = Ring([(ps[:, 1024:2048], [bbuf[2], bbuf[3]]), (ps[:, 3072:4096], [bbuf[6], bbuf[7]])])
            tabs = None
            if final:
                gt = A.alloc([D]); bt = A.alloc([D]); b_tab = Buf("ftab")
                P.op("sp", [DMA(gt, lng_d[l, 1].partition_broadcast(128))], writes=[b_tab], dma=("ftab", 0))
                P.op("sp", [DMA(bt, lnb_d[l, 1].partition_broadcast(128))], writes=[b_tab], dma=("ftab", 1))
                tabs = (gt, bt, b_tab)
            w1v = w1_d[l].rearrange("(k p) n -> p k n", p=128)
            w2v = w2_d[l].rearrange("(j p) n -> p j n", p=128)
            for nb in range(8):
                P.op("pool", [DMA(w1[:, :, nb * 512:(nb + 1) * 512], w1v[:, :, nb * 512:(nb + 1) * 512])],
                     writes=[w1b[nb]], dma=("w1", nb))
            for jg in range(8):
                P.op("pool", [DMA(w2[:, jg * 4:(jg + 1) * 4, :], w2v[:, jg * 4:(jg + 1) * 4, :])],
                     writes=[w2b[jg]], dma=("w2", jg))
            nb_ = len(blocks)
            t0_, n0_, s0_ = blocks[0]
            head_load(src, t0_, n0_, xin)
            head_compute(n0_, s0_, "2", hTs[0][0], hTs[0][1], xaTs[0][0], xaTs[0][1], xin, ring)
            pending = []
            for bi, (tile0, ntok, s) in enumerate(blocks):
                cur = bi % 2
                hT, b_hT = hTs[cur]
                xaT, b_xaT = xaTs[cur]
                if bi + 1 < nb_:
                    head_load(src, blocks[bi + 1][0], blocks[bi + 1][1], xin)
                for j in range(HC):
                    bk, bb_ = ring.next()
                    P.op("pe", [MM(bk[:, 0:ntok], w1[:, k, j * 128:(j + 1) * 128], hT[:, k, 0:ntok], start=(k == 0), stop=(k == 7))
                                for k in range(8)], reads=[b_hT, w1b[j // 4]], writes=[bb_])
                    sqt, sqb = sq.next()
                    P.op("act", [ACTV(sqt[:, 0:ntok], bk[:, 0:ntok], AF.Square)], reads=[bb_], writes=[sqb])
                    P.op("dve", [STT(hid[:, j, 0:ntok], bk[:, 0:ntok], 0.0, sqt[:, 0:ntok], ALU.is_gt, ALU.mult)],
                         reads=[bb_, sqb], writes=[hidb[j]])
                    if j in (1, 15) and pending:
                        pending.pop(0)()
                while pending:
                    pending.pop(0)()
                if bi + 1 < nb_:
                    nx = 1 - cur
                    head_compute(blocks[bi + 1][1], blocks[bi + 1][2], "2", hTs[nx][0], hTs[nx][1], xaTs[nx][0], xaTs[nx][1], xin, ring)
                g2 = V["g2"][s]
                for c in range(8):
                    bk, bb_ = ring.next()
                    P.op("pe", [MM(bk[:, 0:ntok], w2[:, j, c * 128:(c + 1) * 128], hid[:, j, 0:ntok], start=(j == 0), stop=(j == HC - 1))
                                for j in range(HC)], reads=hidb + w2b, writes=[bb_])
                    P.op("dve", [STT(xaT[:, c, 0:ntok], bk[:, 0:ntok], g2[:, c:c + 1], xaT[:, c, 0:ntok], ALU.mult, ALU.add)],
                         reads=[bb_, b_xaT, b_vec], writes=[b_xaT])
                pending = [(lambda t=t, xaT=xaT, b_xaT=b_xaT, tile0=tile0: tail_tile(xaT, b_xaT, t, dst, tile0, xout, tabs, pairs))
                           for t in range(ntok // 128)]
            while pending:
                pending.pop(0)()
            P.barrier()
            A.reset(m)

        def phase_conv(l, src, dst, seqs):
            o = l // 2
            m = A.mark()
            NT = 256
            wci = A.alloc([8, 3 * D], BF16); wcib = [Buf(f"wci{i}") for i in range(6)]
            wco = A.alloc([8, D], BF16); wcob = Buf("wco")
            hT = A.alloc([8, NT], BF16); b_hT = Buf("hT")
            xa_ring = Ring([(A.alloc([8, NT]), Buf(f"xaT{i}")) for i in range(3)])
            gb_ring = Ring([(A.alloc([8, NT]), Buf(f"gb{i}")) for i in range(3)])
            u_ring = Ring([(A.alloc([8, NT + 2]), Buf(f"u{i}")) for i in range(4)])
            gct = Ring([(A.alloc([NT]), Buf(f"gct{i}")) for i in range(2)])
            acc_t = Ring([(A.alloc([NT]), Buf(f"cacc{i}")) for i in range(2)])
            vT = A.alloc([8, NT], BF16); b_vT = Buf("vT")
            xin = [(A.alloc([D]), Buf(f"xin{i}"), i) for i in range(2)]
            xout = Ring([(A.alloc([D]), Buf(f"xout{i}"), i) for i in range(2)])
            ring = Ring([BK[i] for i in (0, 1, 4, 5)])
            pairs = Ring([(ps[:, 1024:2048], [bbuf[2], bbuf[3]]), (ps[:, 3072:4096], [bbuf[6], bbuf[7]])])
            wv = cw_in[o].rearrange("(k p) n -> p k n", p=128)
            for nb in (0, 2, 4, 1, 3, 5):
                P.op("pool", [DMA(wci[:, :, nb * 512:(nb + 1) * 512], wv[:, :, nb * 512:(nb + 1) * 512])],
                     writes=[wcib[nb]], dma=("wci", nb))
            P.op("pool", [DMA(wco, cw_out[o].rearrange("(k p) n -> p k n", p=128))], writes=[wcob], dma=("wco", 0))
            cw0, cw1, cw2, cbv = [convT[:, o, i, :] for i in range(4)]

            def proj(blk, fillers, preloaded):
                tile0, ntok, s = blk
                xaT, b_xaT = xa_ring.next()
                gb, b_gb = gb_ring.next()
                u, b_u = u_ring.next()
                if not preloaded:
                    head_load(src, tile0, ntok, xin)
                head_compute(ntok, s, "1", hT, b_hT, xaT, b_xaT, xin, ring)
                for c in range(8):
                    bk, bb_ = ring.next()
                    P.op("pe", [MM(bk[:, 0:ntok], wci[:, k, c * 128:(c + 1) * 128], hT[:, k, 0:ntok], start=(k == 0), stop=(k == 7))
                                for k in range(8)], reads=[b_hT, wcib[c // 4]], writes=[bb_])
                    P.op("act", [ACTV(gb[:, c, 0:ntok], bk[:, 0:ntok], AF.Copy)], reads=[bb_], writes=[b_gb])
                    bk, bb_ = ring.next()
                    cc = 8 + c
                    P.op("pe", [MM(bk[:, 0:ntok], wci[:, k, cc * 128:(cc + 1) * 128], hT[:, k, 0:ntok], start=(k == 0), stop=(k == 7))
                                for k in range(8)], reads=[b_hT, wcib[cc // 4]], writes=[bb_])
                    g_t, g_b = gct.next()
                    P.op("act", [ACTV(g_t[:, 0:ntok], bk[:, 0:ntok], AF.Copy)], reads=[bb_], writes=[g_b])
                    bk, bb_ = ring.next()
                    cc = 16 + c
                    P.op("pe", [MM(bk[:, 0:ntok], wci[:, k, cc * 128:(cc + 1) * 128], hT[:, k, 0:ntok], start=(k == 0), stop=(k == 7))
                                for k in range(8)], reads=[b_hT, wcib[cc // 4]], writes=[bb_])
                    P.op("dve", [TT(u[:, c, 1:ntok + 1], bk[:, 0:ntok], g_t[:, 0:ntok], ALU.mult)], reads=[bb_, g_b], writes=[b_u])
                    if fillers:
                        fillers[c]()
                return dict(blk=blk, xaT=xaT, b_xaT=b_xaT, gb=gb, b_gb=b_gb, u=u, b_u=b_u)

            def conv_fillers(cur, prev, nxt):
                tile0, ntok, s = cur["blk"]
                u, b_u = cur["u"], cur["b_u"]
                gb, b_gb = cur["gb"], cur["b_gb"]

                def halo():
                    if prev is None:
                        P.op("pool", [MEMSET(u[:, :, 0:1], 0.0)], writes=[b_u])
                    else:
                        pn = prev["blk"][1]
                        P.op("pool", [CP(u[:, :, 0:1], prev["u"][:, :, pn:pn + 1])], reads=[prev["b_u"]], writes=[b_u])
                    if nxt is None:
                        P.op("pool", [MEMSET(u[:, :, ntok + 1:ntok + 2], 0.0)], writes=[b_u])
                    else:
                        P.op("pool", [CP(u[:, :, ntok + 1:ntok + 2], nxt["u"][:, :, 1:2])], reads=[nxt["b_u"]], writes=[b_u])

                def chunk(c):
                    if c == 0:
                        halo()
                    a_t, a_b = acc_t.next()
                    a = a_t[:, 0:ntok]
                    P.op("dve", [TS(a, u[:, c, 1:ntok + 1], cw1[:, c:c + 1], cbv[:, c:c + 1], ALU.mult, ALU.add)],
                         reads=[b_u, b_par], writes=[a_b])
                    P.op("dve", [STT(a, u[:, c, 0:ntok], cw0[:, c:c + 1], a, ALU.mult, ALU.add)], reads=[b_u, b_par, a_b], writes=[a_b])
                    P.op("dve", [STT(a, u[:, c, 2:ntok + 2], cw2[:, c:c + 1], a, ALU.mult, ALU.add)], reads=[b_u, b_par, a_b], writes=[a_b])
                    P.op("pool", [TT(vT[:, c, 0:ntok], a, gb[:, c, 0:ntok], ALU.mult)], reads=[a_b, b_gb], writes=[b_vT])
                return [(lambda c=c: chunk(c)) for c in range(8)]

            def outproj_tail(cur):
                tile0, ntok, s = cur["blk"]
                xaT, b_xaT = cur["xaT"], cur["b_xaT"]
                g1 = V["g1"][s]
                for c in range(8):
                    bk, bb_ = ring.next()
                    P.op("pe", [MM(bk[:, 0:ntok], wco[:, k, c * 128:(c + 1) * 128], vT[:, k, 0:ntok], start=(k == 0), stop=(k == 7))
                                for k in range(8)], reads=[b_vT, wcob], writes=[bb_])
                    P.op("dve", [STT(xaT[:, c, 0:ntok], bk[:, 0:ntok], g1[:, c:c + 1], xaT[:, c, 0:ntok], ALU.mult, ALU.add)],
                         reads=[bb_, b_xaT, b_vec], writes=[b_xaT])
                tail(xaT, b_xaT, ntok, dst, tile0, xout, None, pairs)

            for blocks in seqs:
                n = len(blocks)
                states = []
                for i, blk in enumerate(blocks):
                    fill = None
                    if i >= 2:
                        fill = conv_fillers(states[i - 2], states[i - 3] if i >= 3 else None, states[i - 1])
                    states.append(proj(blk, fill, preloaded=(i > 0)))
                    if i + 1 < n:
                        head_load(src, blocks[i + 1][0], blocks[i + 1][1], xin)
                    if i >= 2:
                        outproj_tail(states[i - 2])
                for k in range(max(0, n - 2), n):
                    for f in conv_fillers(states[k], states[k - 1] if k >= 1 else None, states[k + 1] if k + 1 < n else None):
                        f()
                    outproj_tail(states[k])
            P.barrier()
            A.reset(m)

        ATT = {}

        def alloc_attn():
            ATT["KTa"] = A.alloc([4, NTOK], BF16)
            ATT["KTb"] = A.alloc([2, NTOK], BF16)
            ATT["VA"] = A.alloc([NKT, VA_W], BF16)
            ATT["b_KT"] = [Buf(f"KT{t}") for t in range(NKT)]
            ATT["b_ones"] = Buf("vaones")
            VA_ = ATT["VA"]
            P.op("pool", [MEMSET(VA_[:, :, 0:516].rearrange("p t (h w) -> p t h w", w=129)[:, :, :, 128:129], 1.0),
                          MEMSET(VA_[:, :, 516:646].rearrange("p t (g w) -> p t g w", w=65)[:, :, :, 64:65], 1.0)],
                 writes=[ATT["b_ones"]])

        def rope_ops(src, nvec, rc, rs, t1, t2):
            s3 = src.rearrange("p (v d) -> p v d", v=nvec)
            t13 = t1.rearrange("p (v d) -> p v d", v=nvec)
            rcb = rc.unsqueeze(1).broadcast_to([128, nvec, 64])
            s5 = src.rearrange("p (v a h f) -> p v a h f", v=nvec, a=2, h=2, f=16)
            t25 = t2.rearrange("p (v a h f) -> p v a h f", v=nvec, a=2, h=2, f=16)
            rs4 = rs.rearrange("p (a h f) -> p a h f", a=2, h=2, f=16)
            fns = [TT(t13, s3, rcb, ALU.mult)]
            for h in range(2):
                in1 = rs4[:, :, h, :].unsqueeze(1).broadcast_to([128, nvec, 2, 16])
                fns.append(TT(t25[:, :, :, h, :], s5[:, :, :, 1 - h, :], in1, ALU.mult))
            return fns

        def phase_kv(l, src, blocks):
            e = l // 2
            KTa, KTb, VA, b_KT, b_ones = ATT["KTa"], ATT["KTb"], ATT["VA"], ATT["b_KT"], ATT["b_ones"]
            m = A.mark()
            NT = 512
            wkv = A.alloc([8, 1280], BF16); wkvb = [Buf("wkv0"), Buf("wkv1"), Buf("wkv2")]
            hT = A.alloc([8, NT], BF16); b_hT = Buf("hT")
            xin = [(A.alloc([D]), Buf(f"xin{i}"), i) for i in range(4)]
            ropes = Ring([(A.alloc([64]), A.alloc([64]), Buf(f"rope{i}"), i) for i in range(2)])
            gk = A.alloc([128]); b_g = Buf("gk")
            t1r = Ring([(A.alloc([512]), A.alloc([512]), Buf(f"t12_{i}")) for i in range(2)])
            krr = Ring([(A.alloc([512], BF16), Buf(f"kr{i}")) for i in range(2)])
            sm = Ring([dict(sq=A.alloc([128]), ss=A.alloc([2]), rstd=A.alloc([2]), xg=A.alloc([128]), t1=A.alloc([128]),
                            t2=A.alloc([128]), kbd=A.alloc([256], BF16), buf=Buf(f"sm{i}")) for i in range(2)])
            ring = Ring([BK[i] for i in (0, 1, 2, 3, 4, 5, 6)])
            wv = w_in[e].rearrange("(k p) n -> p k n", p=128)
            P.op("pool", [DMA(wkv[:, :, 0:512], wv[:, :, 512:1024])], writes=[wkvb[0]], dma=("wkv", 0))
            P.op("pool", [DMA(wkv[:, :, 512:1024], wv[:, :, 1024:1536])], writes=[wkvb[1]], dma=("wkv", 1))
            P.op("pool", [DMA(wkv[:, :, 1024:1280], wv[:, :, 2048:2304])], writes=[wkvb[2]], dma=("wkv", 2))
            P.op("sp", [DMA(gk[:, 0:64], kng[e].partition_broadcast(128))], writes=[b_g], dma=("gk", 0))
            P.op("sp", [DMA(gk[:, 64:128], kng[e].partition_broadcast(128))], writes=[b_g], dma=("gk", 1))
            pb7 = banks[7].bitcast(BF16)
            for (tile0, ntok, s) in blocks:
                head(src, tile0, ntok, s, "1", hT, b_hT, None, None, xin, ring)
                for t in range(ntok // 128):
                    kt = tile0 + t
                    rc, rs, rb, ri = ropes.next()
                    P.op("sp", [DMA(rc, ropeC[kt * 128:(kt + 1) * 128, :])], writes=[rb], dma=("ropec", ri))
                    P.op("sp", [DMA(rs, ropeS[kt * 128:(kt + 1) * 128, :])], writes=[rb], dma=("ropes", ri))
                    lhs = [hT[:, k, t * 128:(t + 1) * 128] for k in range(8)]
                    zk, zkb = ring.next()
                    P.op("pe", [MM(zk, lhs[k], wkv[:, k, 0:512], start=(k == 0), stop=(k == 7)) for k in range(8)],
                         reads=[b_hT, wkvb[0]], writes=[zkb])
                    zv, zvb = ring.next()
                    P.op("pe", [MM(zv, lhs[k], wkv[:, k, 512:1024], start=(k == 0), stop=(k == 7)) for k in range(8)],
                         reads=[b_hT, wkvb[1]], writes=[zvb])
                    zb, zbb = ring.next()
                    P.op("pe", [MM(zb[:, 0:256], lhs[k], wkv[:, k, 1024:1280], start=(k == 0), stop=(k == 7)) for k in range(8)],
                         reads=[b_hT, wkvb[2]], writes=[zbb])
                    t1, t2, tb_ = t1r.next()
                    fns = rope_ops(zk, 8, rc, rs, t1, t2)
                    P.op("dve", fns, reads=[zkb, rb], writes=[tb_])
                    kr, krb = krr.next()
                    P.op("pool", [TT(kr, t1, t2, ALU.add)], reads=[tb_], writes=[krb])
                    P.op("pe", [TR(pb7[:, h * 128:(h + 1) * 128], kr[:, h * 128:(h + 1) * 128], identB) for h in range(4)],
                         reads=[krb, b_id], writes=[bbuf[7]])
                    P.op("act", [ACTV(KTa[:, :, kt * 128:(kt + 1) * 128], pb7[:, 0:512].rearrange("p (h t) -> p h t", h=4), AF.Copy)],
                         reads=[bbuf[7]], writes=[b_KT[kt]])
                    P.op("act", [ACTV(VA[:, kt, 0:516].rearrange("p (h w) -> p h w", w=129)[:, :, 0:128],
                                      zv.rearrange("p (h w) -> p h w", w=128), AF.Copy)], reads=[zvb, b_ones], writes=[b_KT[kt]])
                    q = sm.next()
                    P.op("act", [ACTV(q["sq"], zb[:, 0:128], AF.Square)], reads=[zbb], writes=[q["buf"]])
                    P.op("dve", [lambda e_, q=q: e_.tensor_reduce(out=q["ss"], in_=q["sq"].rearrange("p (h d) -> p h d", h=2),
                                                                  axis=AX.X, op=ALU.add)], reads=[q["buf"]], writes=[q["buf"]])
                    P.op("dve", [TS(q["ss"], q["ss"], 1.0 / 64, EPS, ALU.mult, ALU.add)], reads=[q["buf"]], writes=[q["buf"]])
                    P.op("pool", [TT(q["rstd"], q["ss"], neghalf[:, 0:2], ALU.pow)], reads=[q["buf"], b_const], writes=[q["buf"]])
                    P.op("dve", [TT(q["xg"], zb[:, 0:128], gk, ALU.mult)], reads=[zbb, b_g], writes=[q["buf"]])
                    P.op("dve", rope_ops(q["xg"], 2, rc, rs, q["t1"], q["t2"]), reads=[q["buf"], rb], writes=[q["buf"]])
                    P.op("pool", [TT(q["t1"], q["t1"], q["t2"], ALU.add)], reads=[q["buf"]], writes=[q["buf"]])
                    kb4 = q["kbd"].rearrange("p (g u d) -> p g u d", g=2, u=2, d=64)
                    rsb = q["rstd"].unsqueeze(2).broadcast_to([128, 2, 64])
                    t13 = q["t1"].rearrange("p (g d) -> p g d", g=2)
                    P.op("dve", [TT(kb4[:, :, 0, :], t13, rsb, ALU.mult), TT(kb4[:, :, 1, :], t13, rsb, ALU.mult)],
                         reads=[q["buf"]], writes=[q["buf"]])
                    P.op("pe", [TR(pb7[:, 512 + g * 128:512 + (g + 1) * 128], q["kbd"][:, g * 128:(g + 1) * 128], identB) for g in range(2)],
                         reads=[q["buf"], b_id], writes=[bbuf[7]])
                    P.op("act", [ACTV(KTb[:, :, kt * 128:(kt + 1) * 128], pb7[:, 512:768].rearrange("p (g t) -> p g t", g=2), AF.Copy)],
                         reads=[bbuf[7]], writes=[b_KT[kt]])
                    P.op("act", [ACTV(VA[:, kt, 516:646].rearrange("p (g w) -> p g w", w=65)[:, :, 0:64],
                                      zb[:, 128:256].rearrange("p (g w) -> p g w", w=64), AF.Copy)], reads=[zbb, b_ones], writes=[b_KT[kt]])
            P.barrier()
            A.reset(m)

        def phase_attn(l, src, dst, blocks):
            e = l // 2
            lam_init = 0.8 - 0.6 * float(np.exp(-0.3 * l))
            KTa, KTb, VA, b_KT, b_ones = ATT["KTa"], ATT["KTb"], ATT["VA"], ATT["b_KT"], ATT["b_ones"]
            m = A.mark()
            NT = 256
            NQS = 2
            wq = A.alloc([8, 1024], BF16); wqb = [Buf("wq0"), Buf("wq1")]
            wo = A.alloc([8, D], BF16); wob = Buf("wo")
            hT = A.alloc([8, NT], BF16); b_hT = Buf("hT")
            xaT = A.alloc([8, NT]); b_xaT = Buf("xaT")
            xin = [(A.alloc([D]), Buf(f"xin{i}"), i) for i in range(2)]
            xout = Ring([(A.alloc([D]), Buf(f"xout{i}"), i) for i in range(2)])
            ropes = Ring([(A.alloc([64]), A.alloc([64]), Buf(f"rope{i}"), i) for i in range(2)])
            gq = A.alloc([64]); gsub = A.alloc([128]); lvt = A.alloc([256]); lam = A.alloc([8]); b_g = Buf("gq")
            scr = A.alloc([2048])
            t1 = scr[:, 0:512]; t2 = scr[:, 512:1024]; b_t12 = Buf("t12")
            sqq = scr[:, 1024:1536]; ssq = A.alloc([8]); rsq = A.alloc([8]); b_sq = Buf("sqq")
            qr = scr[:, 1536:2048].bitcast(BF16); b_qr = Buf("qr")
            QTz = A.alloc([8, 2, NT], BF16); b_QT = Buf("QT")
            PT = Ring([(A.alloc([512], BF16), Buf(f"PT{i}")) for i in range(5)])
            otok = scr.bitcast(BF16).rearrange("p (q c) -> p q c", q=4); b_otok = [Buf(f"otok{i}") for i in range(4)]
            od = [A.alloc([4, 128]), A.alloc([4, 128])]; b_od = [Buf("od0"), Buf("od1")]
            dd = A.alloc([4, 128]); junk = A.alloc([128]); b_dd = Buf("dd")
            rz_ring = Ring([(A.alloc([1]), Buf(f"rz{i}")) for i in range(4)])
            ssd_ring = Ring([(A.alloc([4]), A.alloc([4]), Buf(f"ssd{i}")) for i in range(2)])
            wv = w_in[e].rearrange("(k p) n -> p k n", p=128)
            P.op("pool", [DMA(wq[:, :, 0:512], wv[:, :, 0:512])], writes=[wqb[0]], dma=("wq", 0))
            P.op("pool", [DMA(wq[:, :, 512:1024], wv[:, :, 1536:2048])], writes=[wqb[1]], dma=("wq", 1))
            P.op("pool", [DMA(wo, w_out[e].rearrange("(k p) n -> p k n", p=128))], writes=[wob], dma=("wo", 0))
            P.op("sp", [DMA(gq, qng[e].partition_broadcast(128))], writes=[b_g], dma=("gq", 0))
            P.op("sp", [DMA(gsub, subg[e].partition_broadcast(128))], writes=[b_g], dma=("gq", 1))
            P.op("sp", [DMA(lvt, dlam[e].partition_broadcast(128))], writes=[b_g], dma=("gq", 2))
            P.op("pool", [MEMSET(QTz, 0.0)], writes=[b_QT])
            P.op("dve", [lambda e_: e_.scalar_tensor_tensor(out=junk[:, 0:64], in0=lvt[:, 0:64], scalar=1.0, in1=lvt[:, 64:128],
                                                            op0=ALU.mult, op1=ALU.mult, accum_out=lam[:, 0:1])], reads=[b_g], writes=[b_dd])
            P.op("dve", [lambda e_: e_.scalar_tensor_tensor(out=junk[:, 0:64], in0=lvt[:, 128:192], scalar=1.0, in1=lvt[:, 192:256],
                                                            op0=ALU.mult, op1=ALU.mult, accum_out=lam[:, 1:2])], reads=[b_g, b_dd], writes=[b_dd])
            P.op("act", [ACTV(lam[:, 3:5], lam[:, 0:2], AF.Exp)], reads=[b_dd], writes=[b_dd])
            P.op("dve", [STT(lam[:, 2:3], lam[:, 4:5], -lam_init, lam[:, 3:4], ALU.add, ALU.subtract)], reads=[b_dd], writes=[b_dd])
            P.op("dve", [TS(gsub, gsub, 1.0 - lam_init, None, ALU.mult)], reads=[b_g], writes=[b_g])
            nlam = lam[:, 2:3]

            def acc_region(s_, r):
                if r < 3:
                    b = 4 + s_
                    return banks[b][:, r * 129:(r + 1) * 129], bbuf[b]
                return banks[6][:, s_ * 129:(s_ + 1) * 129], bbuf[6]
            P.op("dve", [MEMSET(banks[4], 0.0), MEMSET(banks[5], 0.0), MEMSET(banks[6], 0.0)], writes=[bbuf[4], bbuf[5], bbuf[6]])
            pb7 = banks[7].bitcast(BF16)
            Sring = Ring([BK[0], BK[1], BK[2], BK[3], BK[7]])
            ring01 = Ring([BK[0], BK[1]])
            LOOK = 3
            for (tile0, ntok, s) in blocks:
                assert ntok == NT
                keytiles = [0, 1] if s == 1 else list(range(NKT))
                head(src, tile0, ntok, s, "1", hT, b_hT, xaT, b_xaT, xin, ring01)
                for t in range(NQS):
                    kt = tile0 + t
                    rc, rs, rb, ri = ropes.next()
                    P.op("sp", [DMA(rc, ropeC[kt * 128:(kt + 1) * 128, :])], writes=[rb], dma=("ropec", ri))
                    P.op("sp", [DMA(rs, ropeS[kt * 128:(kt + 1) * 128, :])], writes=[rb], dma=("ropes", ri))
                    lhs = [hT[:, k, t * 128:(t + 1) * 128] for k in range(8)]
                    za, zab = BK[2]
                    zq, zqb = BK[3]
                    P.op("pe", [MM(za, lhs[k], wq[:, k, 0:512], start=(k == 0), stop=(k == 7)) for k in range(8)],
                         reads=[b_hT, wqb[0]], writes=[zab])
                    P.op("pe", [MM(zq, lhs[k], wq[:, k, 512:1024], start=(k == 0), stop=(k == 7)) for k in range(8)],
                         reads=[b_hT, wqb[1]], writes=[zqb])
                    P.op("dve", rope_ops(za, 8, rc, rs, t1, t2), reads=[zab, rb], writes=[b_t12])
                    P.op("pool", [TT(qr[:, 0:512], t1, t2, ALU.add)], reads=[b_t12], writes=[b_qr])
                    P.op("act", [ACTV(sqq, zq, AF.Square)], reads=[zqb], writes=[b_sq])
                    P.op("dve", [lambda e_: e_.tensor_reduce(out=ssq, in_=sqq.rearrange("p (h d) -> p h d", h=8), axis=AX.X, op=ALU.add)],
                         reads=[b_sq], writes=[b_sq])
                    P.op("dve", [TS(ssq, ssq, 1.0 / 64, EPS, ALU.mult, ALU.add)], reads=[b_sq], writes=[b_sq])
                    P.op("pool", [TT(rsq, ssq, neghalf[:, 0:8], ALU.pow)], reads=[b_sq, b_const], writes=[b_sq])
                    gqb = gq.unsqueeze(1).broadcast_to([128, 8, 64])
                    P.op("dve", [TT(sqq.rearrange("p (h d) -> p h d", h=8), zq.rearrange("p (h d) -> p h d", h=8), gqb, ALU.mult)],
                         reads=[zqb, b_g, b_sq], writes=[b_sq])
                    P.op("dve", rope_ops(sqq, 8, rc, rs, t1, t2), reads=[b_sq, rb, b_t12], writes=[b_t12])
                    P.op("pool", [TT(t1, t1, t2, ALU.add)], reads=[b_t12], writes=[b_t12])
                    P.op("dve", [TT(qr[:, 512:1024].rearrange("p (h d) -> p h d", h=8), t1.rearrange("p (h d) -> p h d", h=8),
                                    rsq.unsqueeze(2).broadcast_to([128, 8, 64]), ALU.mult)], reads=[b_t12, b_sq], writes=[b_qr])
                    P.op("pe", [TR(pb7[:, i * 128:(i + 1) * 128], qr[:, i * 128:(i + 1) * 128], identB) for i in range(8)],
                         reads=[b_qr, b_id], writes=[bbuf[7]])
                    pb3 = pb7.rearrange("p (i t) -> p i t", i=8)
                    P.op("act", [ACTV(QTz[0:64, :, 0, t * 128:(t + 1) * 128], pb3[0:64], AF.Copy),
                                 ACTV(QTz[64:128, :, 1, t * 128:(t + 1) * 128], pb3[64:128], AF.Copy)],
                         reads=[bbuf[7]], writes=[b_QT])
                steps = [(i, kt) for i in range(8) for kt in keytiles]

                def emit_qk(step):
                    i, kt = step
                    KTx = KTa[:, i, kt * 128:(kt + 1) * 128] if i < 4 else KTb[:, (i - 4) // 2, kt * 128:(kt + 1) * 128]
                    sb, sbb = Sring.next()
                    P.op("pe", [MM(sb, KTx, QTz[:, i, :, :].rearrange("p a q -> p (a q)"))], reads=[b_KT[kt], b_QT], writes=[sbb])
                    pt, ptb = PT.next()
                    P.op("act", [ACTV(pt, sb, AF.Exp, scale=0.125)], reads=[sbb], writes=[ptb])
                    return (step, pt, ptb)

                def emit_pv(item):
                    (i, kt), pt, ptb = item
                    mset = i % 2
                    if i < 4:
                        voff, W = i * 129, 129
                    else:
                        voff, W = 516 + ((i - 4) // 2) * 65, 65
                    fns = []
                    wr = []
                    for r in range(4):
                        reg, rb_ = acc_region(mset, r)
                        fns.append(MM(reg[:, 0:W], pt[:, r * 128:(r + 1) * 128], VA[:, kt, voff:voff + W],
                                      start=False, stop=False, skip_group_check=True))
                        if rb_ not in wr:
                            wr.append(rb_)
                    P.op("pe", fns, reads=[ptb, b_KT[kt]], writes=wr)

                def emit_evac(i):
                    mset = i % 2
                    for r in reversed(range(4)):
                        a, qs = divmod(r, 2)
                        reg, rb_ = acc_region(mset, r)
                        rz, rzb = rz_ring.next()
                        if i < 4:
                            P.op("dve", [RECIP(rz, reg[:, 128:129])], reads=[rb_], writes=[rzb])
                            P.op("dve", [TS(od[a][:, qs, :], reg[:, 0:128], rz[:, 0:1], None, ALU.mult)], reads=[rb_, rzb], writes=[b_od[a]])
                            P.op("dve", [MEMSET(reg[:, 0:129], 0.0)], writes=[rb_])
                        else:
                            head_ = 2 * (i - 4) + a
                            P.op("dve", [RECIP(rz, reg[:, 64:65])], reads=[rb_], writes=[rzb])
                            P.op("dve", [TS(otok[:, qs, 512 + head_ * 64:512 + (head_ + 1) * 64], reg[:, 0:64], rz[:, 0:1], None, ALU.mult)],
                                 reads=[rb_, rzb], writes=[b_otok[qs]])
                            P.op("dve", [MEMSET(reg[:, 0:65], 0.0)], writes=[rb_])
                    if i < 4:
                        ss_, rinv, sb_ = ssd_ring.next()
                        for qs in range(NQS):
                            P.op("dve", [STT(dd[:, qs, :], od[1][:, qs, :], nlam, od[0][:, qs, :], ALU.mult, ALU.add)],
                                 reads=[b_od[0], b_od[1]], writes=[b_dd])
                        for qs in range(NQS):
                            P.op("dve", [lambda e_, ss_=ss_, qs=qs: e_.scalar_tensor_tensor(out=junk, in0=dd[:, qs, :], scalar=1.0, in1=dd[:, qs, :],
                                                                                      op0=ALU.mult, op1=ALU.mult, accum_out=ss_[:, qs:qs + 1])],
                                 reads=[b_dd], writes=[sb_, b_dd])
                        P.op("dve", [TS(ss_[:, 0:NQS], ss_[:, 0:NQS], 1.0 / 128, EPS, ALU.mult, ALU.add)], reads=[sb_], writes=[sb_])
                        P.op("pool", [TT(rinv[:, 0:NQS], ss_[:, 0:NQS], neghalf[:, 0:NQS], ALU.pow)], reads=[sb_, b_const], writes=[sb_])
                        for qs in range(NQS):
                            P.op("dve", [STT(otok[:, qs, i * 128:(i + 1) * 128], dd[:, qs, :], rinv[:, qs:qs + 1], gsub, ALU.mult, ALU.mult)],
                                 reads=[b_dd, sb_, b_g], writes=[b_otok[qs]])

                pend = []
                for n, step in enumerate(steps):
                    pend.append(emit_qk(step))
                    if len(pend) > LOOK:
                        it = pend.pop(0)
                        emit_pv(it)
                        if it[0][1] == keytiles[-1]:
                            emit_evac(it[0][0])
                while pend:
                    it = pend.pop(0)
                    emit_pv(it)
                    if it[0][1] == keytiles[-1]:
                        emit_evac(it[0][0])
                oT = hT
                for qs in range(NQS):
                    P.op("pe", [TR(pb7[:, c * 128:(c + 1) * 128], otok[:, qs, c * 128:(c + 1) * 128], identB) for c in range(8)],
                         reads=[b_otok[qs], b_id], writes=[bbuf[7]])
                    P.op("act", [ACTV(oT[:, :, qs * 128:(qs + 1) * 128], pb7.rearrange("p (c t) -> p c t", c=8), AF.Copy)],
                         reads=[bbuf[7]], writes=[b_hT])
                g1 = V["g1"][s]
                for c in range(8):
                    bk, bb_ = ring01.next()
                    P.op("pe", [MM(bk[:, 0:ntok], wo[:, k, c * 128:(c + 1) * 128], oT[:, k, 0:ntok], start=(k == 0), stop=(k == 7))
                                for k in range(8)], reads=[b_hT, wob], writes=[bb_])
                    P.op("dve", [STT(xaT[:, c, 0:ntok], bk[:, 0:ntok], g1[:, c:c + 1], xaT[:, c, 0:ntok], ALU.mult, ALU.add)],
                         reads=[bb_, b_xaT, b_vec], writes=[b_xaT])
                tail(xaT, b_xaT, ntok, dst, tile0, xout)
            P.barrier()
            A.reset(m)

        layer_vectors(0)
        mA = A.mark()
        alloc_attn()
        phase_kv(0, S_in, CTXB + LAT512)
        if stop_after == "kv0":
            P.emit(); return nc
        phase_attn(0, S_in, S_A, CTXB + LAT256)
        A.reset(mA)
        if stop_after == "attn0":
            P.emit(); return nc
        phase_mlp(0, S_A, S_B, CTXB + LAT256, False)
        if stop_after == "l0":
            P.emit(); return nc
        layer_vectors(1)
        phase_conv(1, S_B, S_A, [CTXB, LAT256])
        if stop_after == "conv1":
            P.emit(); return nc
        phase_mlp(1, S_A, S_B, CTXB + LAT256, False)
        layer_vectors(2)
        mA = A.mark()
        alloc_attn()
        phase_kv(2, S_B, CTXB + LAT512)
        phase_attn(2, S_B, S_A, LAT256)
        A.reset(mA)
        phase_mlp(2, S_A, S_B, LAT256, False)
        layer_vectors(3)
        phase_conv(3, S_B, S_A, [LAT256])
        phase_mlp(3, S_A, S_out, LAT256, True)
        P.emit()
        print("arena high-water words:", A.hi, "of", NWORDS, "ops:", P.nops)
    return nc


def _rope_tables():
    f32 = np.float32
    rows = S // 64
    row = np.repeat(np.arange(rows, dtype=f32), 64)
    col = np.tile(np.arange(64, dtype=f32), rows)
    inv_freq = (f32(10000.0) ** (-(np.arange(16, dtype=f32) / f32(16)))).astype(f32)
    ang = np.stack([row, col], axis=-1)[:, :, None].astype(f32) * inv_freq
    c = np.cos(ang).astype(f32)
    s = np.sin(ang).astype(f32)
    C = np.stack([c, c], axis=2).reshape(S, 64)
    Sg = np.stack([-s, s], axis=2).reshape(S, 64)
    C = np.concatenate([np.ones((CTX, 64), f32), C], axis=0)
    Sg = np.concatenate([np.zeros((CTX, 64), f32), Sg], axis=0)
    return np.ascontiguousarray(C), np.ascontiguousarray(Sg)


def make_in_maps(inputs):
    f32 = np.float32
    g = {k: np.asarray(v) for k, v in inputs.items()}
    ropeC_, ropeS_ = _rope_tables()
    lnT = np.stack([g["ln_g"].reshape(4, 2, 8, 128), g["ln_b"].reshape(4, 2, 8, 128)], axis=2)
    lnT = np.ascontiguousarray(lnT.transpose(4, 0, 1, 2, 3).reshape(128, 128)).astype(f32)
    cv = np.concatenate([g["conv_w"], g["conv_b"][:, None, :]], axis=1)
    convT = np.ascontiguousarray(cv.reshape(2, 4, 8, 128).transpose(3, 0, 1, 2).reshape(128, 64)).astype(f32)
    shared = {
        "ada_w": g["ada_w"], "ada_b": g["ada_b"], "attn_w_in": g["attn_w_in"], "attn_w_out": g["attn_w_out"],
        "diff_lambda": np.ascontiguousarray(g["diff_lambda"].reshape(2, 256)), "diff_subln_g": g["diff_subln_g"],
        "q_norm_g": g["q_norm_g"], "k_norm_g": g["k_norm_g"], "conv_w_in": g["conv_w_in"], "conv_w_out": g["conv_w_out"],
        "convT": convT, "mlp_w1": g["mlp_w1"], "mlp_w2": g["mlp_w2"], "lnT": lnT, "ln_g": g["ln_g"], "ln_b": g["ln_b"],
        "ropeC": ropeC_, "ropeS": ropeS_, "ident": np.eye(128, dtype=f32),
    }
    shared = {k: np.ascontiguousarray(v, dtype=f32) for k, v in shared.items()}
    maps = []
    for b in range(8):
        d = dict(shared)
        d["xs"] = np.ascontiguousarray(np.concatenate([g["ctx"][b], g["x"][b]], axis=0), dtype=f32)
        cvec = np.concatenate([g["c"][b].reshape(8, 128).T, g["c_ctx"].reshape(8, 128).T], axis=1)
        d["cvec"] = np.ascontiguousarray(cvec, dtype=f32)
        maps.append(d)
    return maps


_NC_CACHE = {}


def kernel(**inputs):
    maps = make_in_maps(inputs)
    if "nc" not in _NC_CACHE:
        _NC_CACHE["nc"] = build()
    res = run_bass_kernel_spmd(_NC_CACHE["nc"], maps, core_ids=list(range(8)))
    out = np.stack([np.asarray(res.results[b]["out"], dtype=np.float32) for b in range(8)], axis=0)
    return out
```
